# Optimizing a Trainium2 kernel written in Bass

```python
import jax, jax.numpy as jnp
from jax import lax
import numpy as np

D_MODEL = 1024
BATCH = 8
SEQ = 2048
DEPTH = 1
DEC_BATCH = 16
DEC_SEQ = 16
PAST_LEN = 4096

CHUNK = 64
MIX_WIDTH = D_MODEL
RET_WIDTH = MIX_WIDTH // 2
HG_WIDTH = MIX_WIDTH - RET_WIDTH
RET_HEADS = 4
RET_DK = RET_WIDTH // RET_HEADS
RET_DV = RET_WIDTH // RET_HEADS
HG_HEADS = 4
HG_DK = HG_WIDTH // HG_HEADS
HG_DV = HG_WIDTH // HG_HEADS
IN_WIDTH = 4 * RET_WIDTH + 4 * HG_WIDTH
ROPE_THETA = 10000.0
EPS = 1e-6

kernel_name = "hybrid_retention_hgrn2_stream_step"


def _rmsnorm(x, g):
    xf = x.astype(jnp.float32)
    y = xf * lax.rsqrt(jnp.mean(xf * xf, axis=-1, keepdims=True) + EPS)
    return (y * g.astype(jnp.float32)).astype(x.dtype)


def _head_norm(o):
    of = o.astype(jnp.float32)
    return (of * lax.rsqrt(jnp.mean(of * of, axis=-1, keepdims=True) + EPS)).astype(o.dtype)


def _heads(a, n_heads):
    b, t, w = a.shape
    return a.reshape(b, t, n_heads, w // n_heads).transpose(0, 2, 1, 3)


def _merge(a):
    b, h, t, d = a.shape
    return a.transpose(0, 2, 1, 3).reshape(b, t, h * d)


def _rotary(x, pos):
    d = x.shape[-1]
    inv = 1.0 / (ROPE_THETA ** (jnp.arange(0, d, 2, dtype=jnp.float32) / d))
    ang = pos[:, None] * inv[None, :]
    cos = jnp.cos(ang).astype(x.dtype)
    sin = jnp.sin(ang).astype(x.dtype)
    x1, x2 = x[..., : d // 2], x[..., d // 2:]
    return jnp.concatenate([x1 * cos - x2 * sin, x2 * cos + x1 * sin], axis=-1)


def _retention_chunk(S, q, k, v, log_gamma):
    L = q.shape[2]
    idx = jnp.arange(L, dtype=jnp.float32)
    rel = idx[:, None] - idx[None, :]
    mask = rel >= 0
    lg = log_gamma[:, None, None]
    D = jnp.where(mask, jnp.exp(jnp.where(mask, rel, 0.0) * lg), 0.0).astype(q.dtype)
    scores = jnp.einsum('bhtd,bhsd->bhts', q, k) * D[None]
    cross_decay = jnp.exp((idx[None, :] + 1.0) * log_gamma[:, None]).astype(q.dtype)
    o = jnp.einsum('bhts,bhse->bhte', scores, v) + jnp.einsum('bhtd,bhde->bhte', q, S) * cross_decay[None, :, :, None]
    k_decay = jnp.exp((L - 1.0 - idx)[None, :] * log_gamma[:, None]).astype(q.dtype)
    chunk_decay = jnp.exp(L * log_gamma).astype(S.dtype)[None, :, None, None]
    S_new = chunk_decay * S + jnp.einsum('bhsd,bhse->bhde', k * k_decay[None, :, :, None], v)
    return o, S_new


def _hgrn2_chunk(S, q, k, v, logf):
    L = q.shape[2]
    b = jnp.cumsum(logf, axis=2)
    idx = jnp.arange(L)
    mask = (idx[:, None] >= idx[None, :])[:, :, None]
    diff = b[:, :, :, None, :] - b[:, :, None, :, :]
    decay = jnp.exp(jnp.where(mask, diff, -jnp.inf))
    A = jnp.einsum('bhtd,bhsd,bhtsd->bhts', q, k, decay)
    o = jnp.einsum('bhts,bhse->bhte', A, v) + jnp.einsum('bhtd,bhde->bhte', q * jnp.exp(b), S)
    b_last = b[:, :, -1:, :]
    S_new = jnp.exp(b_last[:, :, 0, :])[..., None] * S + jnp.einsum('bhsd,bhse->bhde', k * jnp.exp(b_last - b), v)
    return o, S_new


def _run_mixers(S_ret, S_hg, rq, rk, rv, hq, hk, hv, logf, log_gamma):
    T = rq.shape[2]
    L = min(CHUNK, T)
    n = T // L

    def to_chunks(a):
        b, h, _, d = a.shape
        return a.reshape(b, h, n, L, d).transpose(2, 0, 1, 3, 4)

    def from_chunks(a):
        _, b, h, _, d = a.shape
        return a.transpose(1, 2, 0, 3, 4).reshape(b, h, T, d)

    xs = (to_chunks(rq), to_chunks(rk), to_chunks(rv), to_chunks(hq), to_chunks(hk), to_chunks(hv), to_chunks(logf))

    def body(carry, c):
        Sr, Sh = carry
        cq, ck, cv, dq, dk, dv, lf = c
        o_r, Sr = _retention_chunk(Sr, cq, ck, cv, log_gamma)
        o_h, Sh = _hgrn2_chunk(Sh, dq, dk, dv, lf)
        return (Sr, Sh), (o_r, o_h)

    (S_ret, S_hg), (o_r, o_h) = lax.scan(body, (S_ret, S_hg), xs)
    return from_chunks(o_r), from_chunks(o_h), S_ret, S_hg


def _layer(x, pos, S_ret, S_hg, w_in, w_out, g_pre, g_post, lb):
    h = _rmsnorm(x, g_pre)
    u = jnp.einsum('btd,de->bte', h, w_in)
    cuts = np.cumsum([RET_WIDTH] * 4 + [HG_WIDTH] * 3)
    rq, rk, rv, rg, hq, hf, hi, hg = jnp.split(u, cuts.tolist(), axis=-1)
    rq = _rotary(_heads(rq, RET_HEADS), pos)
    rk = _rotary(_heads(rk, RET_HEADS), pos) * (RET_DK ** -0.5)
    rv = _heads(rv, RET_HEADS)
    log_gamma = jnp.log(1.0 - 2.0 ** (-5.0 - jnp.arange(RET_HEADS, dtype=jnp.float32)))
    hq = jax.nn.silu(_heads(hq, HG_HEADS))
    f = lb + (1.0 - lb) * jax.nn.sigmoid(_heads(hf, HG_HEADS))
    logf = jnp.log(f)
    hk = 1.0 - f
    hv = _heads(hi, HG_HEADS)
    o_r, o_h, S_ret, S_hg = _run_mixers(S_ret, S_hg, rq, rk, rv, hq, hk, hv, logf, log_gamma)
    o = jnp.concatenate([jax.nn.silu(rg) * _merge(_head_norm(o_r)), jax.nn.silu(hg) * _merge(_head_norm(o_h))], axis=-1)
    y = jnp.einsum('bte,ed->btd', o, w_out)
    return x + _rmsnorm(y, g_post), S_ret, S_hg


def setup_inputs(seed: int = 0) -> dict:
    key = jax.random.key(seed)
    ks = jax.random.split(key, 9)
    x_prompt = jax.random.normal(ks[0], (BATCH, SEQ, D_MODEL), jnp.float32)
    x_sample = jax.random.normal(ks[1], (DEC_BATCH, DEC_SEQ, D_MODEL), jnp.float32)
    state_ret = 0.5 * jax.random.normal(ks[2], (DEPTH, DEC_BATCH, RET_HEADS, RET_DK, RET_DV), jnp.float32)
    state_hgrn = 0.3 * jax.random.normal(ks[3], (DEPTH, DEC_BATCH, HG_HEADS, HG_DK, HG_DV), jnp.float32)
    w_in = jax.random.normal(ks[4], (DEPTH, D_MODEL, IN_WIDTH), jnp.float32) * D_MODEL ** -0.5
    w_out = jax.random.normal(ks[5], (DEPTH, MIX_WIDTH, D_MODEL), jnp.float32) * MIX_WIDTH ** -0.5
    g_pre = 1.0 + 0.05 * jax.random.normal(ks[6], (DEPTH, D_MODEL), jnp.float32)
    g_post = 1.0 + 0.05 * jax.random.normal(ks[7], (DEPTH, D_MODEL), jnp.float32)
    lower_bounds = 0.1 * jax.random.normal(ks[8], (DEPTH + 1, HG_WIDTH), jnp.float32)
    return {"x_prompt": x_prompt, "x_sample": x_sample, "state_ret": state_ret, "state_hgrn": state_hgrn,
            "w_in": w_in, "w_out": w_out, "g_pre": g_pre, "g_post": g_post, "lower_bounds": lower_bounds}


def reference(x_prompt, x_sample, state_ret, state_hgrn, w_in, w_out, g_pre, g_post, lower_bounds):
    pos_prompt = jnp.arange(SEQ, dtype=jnp.float32)
    pos_sample = PAST_LEN + jnp.arange(DEC_SEQ, dtype=jnp.float32)
    lb_all = jnp.cumsum(jax.nn.softmax(lower_bounds.astype(jnp.float32), axis=0), axis=0)
    hp, hs = x_prompt, x_sample
    ret_p, ret_s, hg_p, hg_s = [], [], [], []
    for l in range(DEPTH):
        lb = lb_all[l].reshape(HG_HEADS, 1, HG_DK).astype(x_prompt.dtype)
        S_ret0 = jnp.zeros((BATCH, RET_HEADS, RET_DK, RET_DV), x_prompt.dtype)
        S_hg0 = jnp.zeros((BATCH, HG_HEADS, HG_DK, HG_DV), x_prompt.dtype)
        hp, sr_p, sh_p = _layer(hp, pos_prompt, S_ret0, S_hg0, w_in[l], w_out[l], g_pre[l], g_post[l], lb)
        hs, sr_s, sh_s = _layer(hs, pos_sample, state_ret[l], state_hgrn[l], w_in[l], w_out[l], g_pre[l], g_post[l], lb)
        ret_p.append(sr_p)
        ret_s.append(sr_s)
        hg_p.append(sh_p)
        hg_s.append(sh_s)
    new_ret_prompt = jnp.stack(ret_p, axis=0)
    new_ret_sample = jnp.stack(ret_s, axis=0)
    new_hgrn_prompt = jnp.stack(hg_p, axis=0)
    new_hgrn_sample = jnp.stack(hg_s, axis=0)
    return (hp, hs, new_ret_prompt, new_ret_sample, new_hgrn_prompt, new_hgrn_sample)
```

```python
import sys
import numpy as np
from contextlib import ExitStack
import concourse.bass as bass
import concourse.mybir as mybir
from concourse.bass_utils import run_bass_kernel_spmd

F32 = mybir.dt.float32
BF16 = mybir.dt.bfloat16
ALU = mybir.AluOpType
AF = mybir.ActivationFunctionType

PE, ACT, DVE, POOL, SP = "pe", "act", "dve", "pool", "sp"
ENGS = (PE, ACT, DVE, POOL, SP)

D_MODEL = 1024
SEQ = 2048
DEC_SEQ = 16
PAST_LEN = 4096
EPS = 1e-6
N_CORES = 8


class Sched:
    def __init__(self):
        self.ops = []
        self.last_w = {}
        self.readers = {}

    def op(self, eng, fn, reads=(), writes=(), dma_key=None, deps=()):
        oid = len(self.ops)
        d = set(deps)
        for b in reads:
            w = self.last_w.get(b)
            if w is not None:
                d.add(w)
        for b in writes:
            w = self.last_w.get(b)
            if w is not None:
                d.add(w)
            d.update(self.readers.get(b, ()))
        d.discard(oid)
        self.ops.append(dict(eng=eng, fn=fn, deps=d, dma_key=dma_key, sig=False, sem=None, val=0, line=sys._getframe(1).f_lineno,
                             writes=tuple(writes)))
        for b in reads:
            self.readers.setdefault(b, []).append(oid)
        for b in writes:
            self.last_w[b] = oid
            self.readers[b] = []
        return oid

    def estimate_costs(self):
        class _Ins:
            def then_inc(self, *a, **k):
                return self

        class _Fake:
            def __init__(self, eng):
                self.eng = eng
                self.cost = 0.0
                self.lat = 0.0
                self.nbytes = 0

            def __getattr__(self, name):
                def f(*args, **kw):
                    out = kw.get("out", args[0] if args else None)
                    n = 1
                    try:
                        n = int(out.free_size())
                    except Exception:
                        pass
                    e = self.eng
                    if name == "dma_start":
                        try:
                            nbytes = int(out.partition_size()) * n * 4
                        except Exception:
                            nbytes = 4096
                        self.cost += 1200.0 if e == POOL else 120.0
                        self.lat += 2000.0
                        self.nbytes += nbytes
                    elif e == PE:
                        self.cost += 240.0 if n >= 512 else 20.0 + 0.37 * n
                    elif e == ACT:
                        self.cost += 190.0 + 0.833 * n + (91.0 if kw.get("accum_out") is not None else 0.0)
                    elif e == DVE:
                        if name == "tensor_tensor_scan":
                            self.cost += 250.0 + 2.08 * n
                        elif name == "reciprocal":
                            self.cost += 150.0 + 8.3 * n
                        elif name in ("tensor_scalar", "tensor_copy"):
                            self.cost += max(250.0, 170.0 + 0.7 * n)
                        else:
                            self.cost += max(250.0, 130.0 + 1.0 * n)
                    elif e == POOL:
                        if kw.get("op") == ALU.pow:
                            self.cost += 740.0
                        elif name in ("memset", "affine_select"):
                            self.cost += 200.0 + 0.5 * n
                        else:
                            self.cost += 150.0 + 1.9 * n
                    else:
                        self.cost += 50.0
                    return _Ins()
                return f

        for o in self.ops:
            fk = _Fake(o["eng"])
            o["fn"](fk)
            o["cost"] = max(fk.cost, 20.0)
            o["lat"] = fk.lat
            o["nbytes"] = fk.nbytes

    def list_schedule(self):
        PRIO_MODE = 0
        ops = self.ops
        n = len(ops)
        self.estimate_costs()
        succ = [[] for _ in range(n)]
        for i, o in enumerate(ops):
            for d in o["deps"]:
                succ[d].append(i)
        prio = [0.0] * n
        for i in range(n - 1, -1, -1):
            o = ops[i]
            best = 0.0
            for sx in succ[i]:
                if prio[sx] > best:
                    best = prio[sx]
            if PRIO_MODE == 1:
                prio[i] = best + (o["cost"] if o["eng"] == PE else 0.25 * (o["cost"] + o["lat"]))
            elif PRIO_MODE == 2:
                prio[i] = -float(i)
            else:
                prio[i] = best + o["cost"] + o["lat"]
        for i, o in enumerate(ops):
            prio[i] += o.get("boost", 0.0)
        XLAT = 250.0
        WIN = 300.0
        ndeps = [len(o["deps"]) for o in ops]
        ready_t = [0.0] * n
        avail = {e: [] for e in ENGS}
        for i, o in enumerate(ops):
            if ndeps[i] == 0:
                avail[o["eng"]].append(i)
        free_t = {e: 0.0 for e in ENGS}
        dma_free = [0.0, 0.0]
        order = {e: [] for e in ENGS}
        done = 0
        while done < n:
            best = None
            for e in ENGS:
                if not avail[e]:
                    continue
                cand = None
                for i in avail[e]:
                    st_ = max(free_t[e], ready_t[i])
                    key = (st_, -prio[i], i)
                    if cand is None or key < cand[0]:
                        cand = (key, i, st_)
                win = cand[2] + WIN
                for i in avail[e]:
                    st_ = max(free_t[e], ready_t[i])
                    if st_ <= win and prio[i] > prio[cand[1]]:
                        cand = ((st_, -prio[i], i), i, st_)
                if best is None or cand[2] < best[2] or (cand[2] == best[2] and prio[cand[1]] > prio[best[1]]):
                    best = (e, cand[1], cand[2])
            e, i, st_ = best
            o = ops[i]
            avail[e].remove(i)
            fin = st_ + o["cost"]
            o["t_idle"] = st_ - free_t[e]
            o["t_start"], o["t_fin"] = st_, fin
            free_t[e] = fin
            order[e].append(i)
            done += 1
            dlat = o["lat"]
            if o["dma_key"] is not None:
                ch = 0 if e == POOL else 1
                xfer_start = max(fin, dma_free[ch])
                dma_free[ch] = xfer_start + o["nbytes"] / 260.0
                dlat = (dma_free[ch] - fin) + o["lat"]
            for sx in succ[i]:
                lat = dlat + (XLAT if ops[sx]["eng"] != e or o["dma_key"] is not None else 60.0)
                if fin + lat > ready_t[sx]:
                    ready_t[sx] = fin + lat
                ndeps[sx] -= 1
                if ndeps[sx] == 0:
                    avail[ops[sx]["eng"]].append(sx)
        self.est_makespan = max(free_t.values())
        return order

    def add_fillers(self, order, filler):
        fn, dep_op, bank_key, gap_min, margin, fdur, t_max = filler
        ops = self.ops
        pe = order[PE]
        stop_at = None
        for k, i in enumerate(pe):
            if bank_key in ops[i]["writes"]:
                stop_at = k
                break
        if stop_at is None:
            stop_at = len(pe)
        new = []
        prev_fin = ops[dep_op]["t_fin"] + 400.0
        nfill = 0
        for k, i in enumerate(pe):
            if k <= stop_at and ops[i]["t_start"] < t_max:
                gap = ops[i]["t_start"] - prev_fin
                late = ops[i]["t_start"] > 30000.0
                if gap > (2500.0 if late else gap_min):
                    cnt = int((gap - (1200.0 if late else margin)) / fdur)
                    for _ in range(max(0, cnt)):
                        fid = len(ops)
                        ops.append(dict(eng=PE, fn=fn, deps={dep_op}, dma_key=None, sig=False, sem=None, val=0, line=0,
                                        writes=(bank_key,), cost=fdur, lat=0.0, nbytes=0))
                        new.append(fid)
                        nfill += 1
                prev_fin = max(prev_fin, ops[i]["t_fin"])
            new.append(i)
        order[PE] = new
        self.n_fillers = nfill
        return order

    def emit(self, nc, stack, reorder=True, filler=None):
        ops = self.ops
        if reorder:
            order = self.list_schedule()
            if filler is not None:
                order = self.add_fillers(order, filler)
        else:
            order = {e: [i for i, o in enumerate(ops) if o["eng"] == e] for e in ENGS}
        pos = {}
        for e in ENGS:
            for k, i in enumerate(order[e]):
                pos[i] = k
        need = [set() for _ in ops]
        for i, o in enumerate(ops):
            for dpi in o["deps"]:
                dp = ops[dpi]
                if dp["dma_key"] is None and dp["eng"] == o["eng"] and o["dma_key"] is None:
                    if o["eng"] == PE:
                        continue
                need[i].add(dpi)
                dp["sig"] = True
        esem = {e: stack.enter_context(nc.semaphore("sem_" + e)) for e in (PE, ACT, DVE, POOL)}
        dsem, dcount = {}, {}
        ecount = {e: 0 for e in ENGS}
        for i, o in enumerate(ops):
            if o["dma_key"] is not None:
                k = o["dma_key"]
                if k not in dsem:
                    dsem[k] = stack.enter_context(nc.semaphore("dsem%d" % len(dsem)))
                    dcount[k] = 0
                dcount[k] += 16
                o["sem"], o["val"], o["sig"] = dsem[k], dcount[k], True
        for e in ENGS:
            for i in order[e]:
                o = ops[i]
                if o["dma_key"] is None and o["sig"]:
                    ecount[e] += 1
                    o["sem"], o["val"] = esem[e], ecount[e]
        streams = order

        def run(h, e):
            waited = {}
            for i in streams[e]:
                o = ops[i]
                req = {}
                for dpi in need[i]:
                    dp = ops[dpi]
                    key = id(dp["sem"])
                    if key not in req or req[key][1] < dp["val"]:
                        req[key] = (dp["sem"], dp["val"])
                for key, (sem, val) in req.items():
                    if waited.get(key, 0) >= val:
                        continue
                    h.wait_ge(sem, val)
                    waited[key] = val
                ins = o["fn"](h)
                if o["sig"]:
                    ins.then_inc(o["sem"], 16 if o["dma_key"] is not None else 1)

        block = stack.enter_context(nc.Block())

        @block.tensor
        def _(e):
            run(e, PE)

        @block.scalar
        def _(e):
            run(e, ACT)

        @block.vector
        def _(e):
            run(e, DVE)

        @block.gpsimd
        def _(e):
            run(e, POOL)

        @block.sync
        def _(e):
            run(e, SP)


def rsl(start, n):
    stop = start - n
    return slice(start, stop if stop >= 0 else None, -1)


def build_nc(limit=None, marks=None):
    nc = bass.Bass("TRN2", target_bir_lowering=False)

    def din(name, shape):
        return nc.dram_tensor(name, list(shape), F32, kind="ExternalInput").ap()

    def dout(name, shape):
        return nc.dram_tensor(name, list(shape), F32, kind="ExternalOutput").ap()

    xp = din("xp", [SEQ, D_MODEL])
    xsm = din("xsm", [2 * DEC_SEQ, D_MODEL])
    sret = din("sret", [2, 4, 128, 128])
    shg = din("shg", [2, 4, 128, 128])
    w_in = din("w_in", [1024, 4096])
    w_out = din("w_out", [1024, 1024])
    g_pre = din("g_pre", [1, 1024])
    g_post = din("g_post", [1, 1024])
    lbs = din("lbs", [128, 8])
    rot_p = din("rot_p", [128, 2, SEQ])
    rot_s = din("rot_s", [128, 2, 2 * DEC_SEQ])
    dec128 = din("dec128", [128, 2, 4, 128])
    dec16 = din("dec16", [128, 2, 4, 16])
    gl = din("gl", [128, 2, 4])
    yp = dout("yp", [SEQ, D_MODEL])
    ys = dout("ys", [2 * DEC_SEQ, D_MODEL])
    nrp = dout("nrp", [4, 128, 128])
    nhp = dout("nhp", [4, 128, 128])
    nrs = dout("nrs", [2, 4, 128, 128])
    nhs = dout("nhs", [2, 4, 128, 128])

    with ExitStack() as st:
        def sb(name, shape, dt):
            return st.enter_context(nc.sbuf_tensor(name, list(shape), dt))

        Wi = sb("Wi", [128, 8, 4096], BF16)
        Wo = sb("Wo", [128, 8, 1024], BF16)
        gpre = sb("gpre", [128, 1024], F32)
        gpost = sb("gpost", [128, 1024], F32)
        rotb = [sb("rotb%d" % i, [128, 2, 512], F32) for i in range(1)]
        decT = {128: sb("decT128", [128, 2, 4, 128], F32), 16: sb("decT16", [128, 2, 4, 16], F32)}
        glT = sb("glT", [128, 2, 4], F32)
        lbt = sb("lbt", [128, 2, 4], F32)
        lbd = sb("lbd", [128, 4], F32)
        lbc = sb("lbc", [128, 3, 4], F32)
        ident = sb("ident", [128, 128], BF16)
        perm = sb("perm", [128, 128], BF16)
        maskT = sb("maskT", [128, 128], F32)
        zer = sb("zer", [128, 128], BF16)
        mhalf = sb("mhalf", [128, 8], F32)
        xs = [sb("xs%d" % i, [128, 1024], F32) for i in range(2)]
        xr = [sb("xr%d" % i, [128, 1024], F32) for i in range(2)]
        xbf = [sb("xbf%d" % i, [128, 1024], BF16) for i in range(2)]
        junk = sb("junk", [128, 512], BF16)
        permb = junk[:, 0:128]
        ssx = sb("ssx", [128, 2], F32)
        msx = sb("msx", [128, 2], F32)
        rsx = sb("rsx", [128, 2], F32)
        xTs = [sb("xT%d" % i, [128, 8, 512], BF16) for i in range(2)]
        NTS = 2
        tA = [sb("tA%d" % i, [128, 512], F32) for i in range(NTS)]
        tB = [sb("tB%d" % i, [128, 512], F32) for i in range(NTS)]
        tC = [sb("tC%d" % i, [128, 512], F32) for i in range(1)]
        tE = [sb("tE%d" % i, [128, 512], F32) for i in range(1)]
        tF = [sb("tF%d" % i, [128, 512], F32) for i in range(1)]
        qb = [sb("qb%d" % i, [128, 512], BF16) for i in range(NTS)]
        invPL = [sb("invPL%d" % i, [128, 4], F32) for i in range(1)]
        QTs = [sb("QT%d" % i, [128, 8, 512], BF16) for i in range(2)]
        KTs = [sb("KT%d" % i, [128, 8, 512], BF16) for i in range(2)]
        decSs = [sb("decS%d" % i, [128, 4, 8], F32) for i in range(2)]
        V = [sb("V%d" % i, [128, 1024], BF16) for i in range(2)]
        gate = [sb("gate%d" % i, [128, 1024], BF16) for i in range(2)]
        Ktm = [sb("Ktm%d" % i, [128, 1024], BF16) for i in range(1)]
        AT = [sb("AT%d" % i, [128, 8, 128], BF16) for i in range(1)]
        Sst_p = sb("Sst", [128, 1024], F32)
        Sb_p = sb("Sb", [128, 1024], BF16)
        ssqs = [sb("ssq%d" % i, [128, 8], F32) for i in range(2)]
        msqs = [sb("msq%d" % i, [128, 8], F32) for i in range(2)]
        rsqs = [sb("rsq%d" % i, [128, 8], F32) for i in range(2)]
        ogs = [sb("og%d" % i, [128, 1024], BF16) for i in range(2)]
        ogT = sb("ogT", [128, 8, 128], BF16)
        ss2 = sb("ss2", [128, 2], F32)
        ms2 = sb("ms2", [128, 1], F32)
        rs2 = sb("rs2", [128, 1], F32)
        banks = [st.enter_context(nc.psum_tensor("bank%d" % i, [128, 512], F32)) for i in range(8)]
        BT, BP0, BP1, BR, BSC, BO0, BO1, BY0 = range(8)
        BDS = BSC
        BOs = (BO0, BO1)
        BY1 = BR
        Tbf = banks[BT][:].bitcast(BF16)

        S = Sched()
        state = dict(pbank=0, tset=0, stores=[], bp=0)

        PPOOL = (BP0, BP1)

        def next_pbank():
            state["pbank"] = (state["pbank"] + 1) % len(PPOOL)
            return PPOOL[state["pbank"]]

        def next_tset():
            state["tset"] = (state["tset"] + 1) % NTS
            return state["tset"]

        S.op(POOL, lambda e: e.memset(ident[:], 1.0), writes=["ident"])
        ident_ready = S.op(POOL, lambda e: e.affine_select(out=ident[:], in_=ident[:], pattern=[[-1, 128]], compare_op=ALU.is_equal,
                                                           fill=0.0, base=0, channel_multiplier=1), reads=["ident"], writes=["ident"])
        S.op(POOL, lambda e: e.memset(perm[:], 1.0), writes=["perm"])
        S.op(POOL, lambda e: e.affine_select(out=perm[:], in_=perm[:], pattern=[[-1, 128]], compare_op=ALU.is_equal,
                                             fill=0.0, base=-64, channel_multiplier=1), reads=["perm"], writes=["perm"])
        S.op(POOL, lambda e: e.memset(permb, 1.0), writes=["permb"] + [("junk", r) for r in range(4)])
        S.op(POOL, lambda e: e.affine_select(out=permb, in_=permb, pattern=[[-1, 128]], compare_op=ALU.is_equal,
                                             fill=0.0, base=64, channel_multiplier=1), reads=["permb"], writes=["permb"] + [("junk", r) for r in range(4)])
        S.op(POOL, lambda e: e.tensor_tensor(out=perm[:], in0=perm[:], in1=permb, op=ALU.add),
             reads=["perm", "permb"] + [("junk", r) for r in range(4)], writes=["perm"])
        S.op(POOL, lambda e: e.memset(maskT[:], 1.0), writes=["maskT"])
        S.op(POOL, lambda e: e.affine_select(out=maskT[:], in_=maskT[:], pattern=[[1, 128]], compare_op=ALU.is_ge,
                                             fill=0.0, base=0, channel_multiplier=-1), reads=["maskT"], writes=["maskT"])
        S.op(POOL, lambda e: e.memset(zer[:], 0.0), writes=["zer"])
        S.op(POOL, lambda e: e.memset(mhalf[:], -0.5), writes=["mhalf"])
        for i in range(1):
            S.op(POOL, lambda e, i=i: e.memset(tF[i][:], 1.0), writes=[("tF", i)])
        w_in_v = w_in.rearrange("(kc p) e -> p kc e", p=128)
        late_w = []
        all_w = []
        WIN_W = 4
        for k, seg in enumerate((0, 1, 5, 4, 2, 3, 6, 7)):
            pieces = ((0, 256), (256, 512)) if k < 2 else ((0, 512),)
            for (c0, c1) in pieces:
                wid = S.op(POOL, lambda e, seg=seg, c0=c0, c1=c1: e.dma_start(out=Wi[:, :, seg * 512 + c0:seg * 512 + c1],
                                                                               in_=w_in_v[:, :, seg * 512 + c0:seg * 512 + c1]),
                           writes=[("Wi", seg, q) for q in range(c0 // 128, c1 // 128)], dma_key=("Wi", seg, c0),
                           deps=all_w[-WIN_W:-WIN_W + 1] if len(all_w) >= WIN_W else [])
                all_w.append(wid)
                if k >= 2:
                    late_w.append(wid)
        w_out_v = w_out.rearrange("(kc p) e -> p kc e", p=128)
        for n in range(2):
            wid = S.op(POOL, lambda e, n=n: e.dma_start(out=Wo[:, :, n * 512:(n + 1) * 512], in_=w_out_v[:, :, n * 512:(n + 1) * 512]),
                       writes=[("Wo", n)], dma_key=("Wo", n), deps=all_w[-WIN_W:-WIN_W + 1])
            all_w.append(wid)
            late_w.append(wid)
        gid = S.op(SP, lambda e: e.dma_start(out=gpre[:], in_=g_pre.partition_broadcast(128)), writes=["gpre"], dma_key="gpre")
        S.ops[gid]["boost"] = 4.5e6
        late_dma = []
        late_dma.append(S.op(SP, lambda e: e.dma_start(out=gpost[:], in_=g_post.partition_broadcast(128)), writes=["gpost"], dma_key="gpost"))
        S.op(SP, lambda e: e.dma_start(out=decT[128][:], in_=dec128), writes=[("dec", 128)], dma_key="dec128")
        late_dma.append(S.op(SP, lambda e: e.dma_start(out=decT[16][:], in_=dec16), writes=[("dec", 16)], dma_key="dec16"))
        S.op(SP, lambda e: e.dma_start(out=glT[:], in_=gl), writes=["glT"], dma_key="glT")

        S.op(SP, lambda e: e.dma_start(out=lbt[:], in_=lbs.rearrange("p (r h) -> p r h", r=2)), writes=["lbt"], dma_key="lbt")
        S.op(DVE, lambda e: e.tensor_tensor(out=lbd[:], in0=lbt[:, 0, :], in1=lbt[:, 1, :], op=ALU.subtract),
             reads=["lbt"], writes=["lbd"])
        S.op(ACT, lambda e: e.activation(out=lbd[:], in_=lbd[:], func=AF.Tanh, scale=0.5), reads=["lbd"], writes=["lbd"])
        S.op(DVE, lambda e: e.tensor_scalar(out=lbc[:, 0, :], in0=lbd[:], scalar1=0.25, scalar2=0.75, op0=ALU.mult, op1=ALU.add),
             reads=["lbd"], writes=["lbc0"])
        S.op(DVE, lambda e: e.tensor_scalar(out=lbc[:, 1, :], in0=lbd[:], scalar1=-0.25, scalar2=0.25, op0=ALU.mult, op1=ALU.add),
             reads=["lbd"], writes=["lbc1"])
        S.op(DVE, lambda e: e.tensor_scalar(out=lbc[:, 2, :], in0=lbd[:], scalar1=0.25, scalar2=-0.25, op0=ALU.mult, op1=ALU.add),
             reads=["lbd"], writes=["lbc2"])
        LBC = ["lbc0", "lbc1", "lbc2"]

        def mark(name):
            if marks is not None:
                marks[name] = len(S.ops)

        def set_ret_dec(li, nch, bp):
            decS = decSs[bp]
            for c in range(nch):
                S.op(DVE, lambda e, c=c: e.tensor_copy(out=decS[:, c, 0:4], in_=glT[:, li, :]),
                     reads=["glT"], writes=[("decS", bp, c, h) for h in range(4)])

        def stage_A(xrows, L, slot, j, xbuf=None, xkey=None):
            bp = state["bp"]
            xT = xTs[bp]
            if xbuf is None:
                xbuf, xkey = xs[slot], ("xs", slot)
            xid = S.op(SP, lambda e: e.dma_start(out=xbuf[0:L, :], in_=xrows), writes=[xkey], dma_key=xkey)
            state["nx"] = state.get("nx", 0) + 1
            if state["nx"] <= 4:
                S.ops[xid]["boost"] = 5e6 - 2e5 * (state["nx"] - 1) if state["nx"] == 1 else 4.4e6 - 1e5 * state["nx"]
            S.op(ACT, lambda e: e.activation(out=xbf[slot][0:L, :], in_=xbuf[0:L, :], func=AF.Square,
                                             accum_out=ssx[0:L, slot:slot + 1]),
                 reads=[xkey], writes=[("ssx", slot), ("xbf", slot)])
            S.op(POOL, lambda e: e.tensor_scalar(out=msx[0:L, slot:slot + 1], in0=ssx[0:L, slot:slot + 1],
                                                scalar1=1.0 / D_MODEL, scalar2=EPS, op0=ALU.mult, op1=ALU.add),
                 reads=[("ssx", slot)], writes=[("msx", slot)])
            pid = S.op(POOL, lambda e: e.tensor_tensor(out=rsx[0:L, slot:slot + 1], in0=msx[0:L, slot:slot + 1],
                                                       in1=mhalf[0:L, 0:1], op=ALU.pow),
                       reads=[("msx", slot), "mhalf"], writes=[("rsx", slot)])
            state["npow"] = state.get("npow", 0) + 1
            if state["npow"] <= 2:
                for k, wid in enumerate(late_w):
                    if (state["npow"] == 1 and k < 4) or (state["npow"] == 2 and k >= 4):
                        S.ops[wid]["deps"].add(pid)
            S.op(DVE, lambda e: e.scalar_tensor_tensor(out=xbf[slot][0:L, :], in0=xbuf[0:L, :], scalar=rsx[0:L, slot:slot + 1],
                                                       in1=gpre[0:L, :], op0=ALU.mult, op1=ALU.mult),
                 reads=[xkey, ("rsx", slot), "gpre"], writes=[("xbf", slot)])

            def tr(e):
                ins = None
                for kc in range(8):
                    ins = e.transpose(out=Tbf[:, kc * L:(kc + 1) * L], in_=xbf[slot][0:L, kc * 128:(kc + 1) * 128],
                                      identity=ident[0:L, 0:L])
                return ins
            S.op(PE, tr, reads=[("xbf", slot), "ident"], writes=["bT"])
            S.op(ACT, lambda e: e.activation(out=xT[:, :, j * L:(j + 1) * L],
                                             in_=Tbf[:, 0:8 * L].rearrange("p (k l) -> p k l", k=8), func=AF.Copy),
                 reads=["bT"], writes=[("xT", bp, j)])

        def proj_fm(col0, seg, nt, nch):
            bk = next_pbank()
            bp = state["bp"]
            xT = xTs[bp]

            def mm(e):
                ins = None
                for kc in range(8):
                    ins = e.matmul(banks[bk][:, 0:nt], lhsT=Wi[:, kc, col0:col0 + 128], rhs=xT[:, kc, 0:nt],
                                   start=(kc == 0), stop=(kc == 7))
                return ins
            S.op(PE, mm, reads=[("Wi", seg, (col0 % 512) // 128)] + [("xT", bp, j) for j in range(nch)], writes=[("bank", bk)])
            return bk

        def stage_B_ret(h, which, nt, L, nch, rslot):
            seg = which
            bk = proj_fm(seg * 512 + h * 128, seg, nt, nch)
            ts = next_tset()
            bp = state["bp"]
            dst = QTs[bp] if which == 0 else KTs[bp]
            dkey = ("QT", bp, h) if which == 0 else ("KT", bp, h)
            br = bk
            gtab = decT[L][:, which, h, 0:L].unsqueeze(1).to_broadcast([128, nch, L])
            S.op(DVE, lambda e: e.tensor_tensor(out=qb[ts][:, 0:nt].rearrange("p (c l) -> p c l", c=nch),
                                                in0=banks[bk][:, 0:nt].rearrange("p (c l) -> p c l", c=nch),
                                                in1=gtab, op=ALU.mult),
                 reads=[("bank", bk), ("dec", L)], writes=[("qb", ts)])
            S.op(PE, lambda e: e.matmul(banks[br][:, 0:nt], lhsT=perm[:], rhs=qb[ts][:, 0:nt], start=True, stop=True),
                 reads=[("qb", ts), "perm"], writes=[("bank", br)])
            S.op(POOL, lambda e: e.tensor_tensor(out=tA[ts][:, 0:nt], in0=qb[ts][:, 0:nt], in1=rotb[rslot][:, 0, 0:nt], op=ALU.mult),
                 reads=[("qb", ts), ("rot", rslot)], writes=[("tA", ts)])
            S.op(DVE, lambda e: e.tensor_tensor(out=tB[ts][:, 0:nt], in0=banks[br][:, 0:nt], in1=rotb[rslot][:, 1, 0:nt], op=ALU.mult),
                 reads=[("bank", br), ("rot", rslot)], writes=[("tB", ts)])
            S.op(POOL, lambda e: e.tensor_tensor(out=dst[:, h, 0:nt], in0=tA[ts][:, 0:nt], in1=tB[ts][:, 0:nt], op=ALU.add),
                 reads=[("tA", ts), ("tB", ts)], writes=[dkey])

        def stage_B_hg(h, nt, L, nch):
            hh = 4 + h
            ts = next_tset()
            bp = state["bp"]
            QT, KT, decS = QTs[bp], KTs[bp], decSs[bp]
            bk = proj_fm(5 * 512 + h * 128, 5, nt, nch)
            S.op(ACT, lambda e: e.activation(out=tC[0][:, 0:nt], in_=banks[bk][:, 0:nt], func=AF.Tanh, scale=0.5),
                 reads=[("bank", bk)], writes=[("tC", 0)])
            S.op(DVE, lambda e: e.tensor_scalar(out=tC[0][:, 0:nt], in0=tC[0][:, 0:nt], scalar1=lbc[:, 1, h:h + 1],
                                                scalar2=lbc[:, 0, h:h + 1], op0=ALU.mult, op1=ALU.add),
                 reads=[("tC", 0)] + LBC, writes=[("tC", 0)])
            for c in range(nch):
                S.op(DVE, lambda e, c=c: e.tensor_tensor_scan(out=tE[0][:, c * L:(c + 1) * L], data0=tC[0][:, c * L:(c + 1) * L],
                                                              data1=zer[:, 0:L], initial=1.0, op0=ALU.mult, op1=ALU.add),
                     reads=[("tC", 0), "zer"], writes=[("tE", 0)])
            for c in range(nch):
                S.op(DVE, lambda e, c=c: e.tensor_tensor_scan(out=tF[0][:, rsl(c * L + L - 2, L - 1)],
                                                              data0=tC[0][:, rsl(c * L + L - 1, L - 1)],
                                                              data1=zer[:, 0:L - 1], initial=1.0, op0=ALU.mult, op1=ALU.add),
                     reads=[("tC", 0), "zer"], writes=[("tF", 0)])
            S.op(DVE, lambda e: e.tensor_scalar(out=tC[0][:, 0:nt], in0=tC[0][:, 0:nt], scalar1=-1.0,
                                                scalar2=1.0, op0=ALU.mult, op1=ALU.add),
                 reads=[("tC", 0)] + LBC, writes=[("tC", 0)])
            S.op(POOL, lambda e: e.tensor_tensor(out=KT[:, hh, 0:nt], in0=tC[0][:, 0:nt], in1=tF[0][:, 0:nt], op=ALU.mult),
                 reads=[("tC", 0), ("tF", 0)], writes=[("KT", bp, hh)])
            S.op(DVE, lambda e: e.tensor_copy(out=decS[:, 0:nch, hh],
                                              in_=tE[0][:, 0:nt].rearrange("p (c l) -> p c l", c=nch)[:, :, L - 1]),
                 reads=[("tE", 0)], writes=[("decS", bp, c, hh) for c in range(nch)])
            S.op(DVE, lambda e: e.reciprocal(out=invPL[0][:, 0:nch], in_=decS[:, 0:nch, hh]),
                 reads=[("decS", bp, c, hh) for c in range(nch)], writes=[("invPL", 0)])
            bk2 = proj_fm(4 * 512 + h * 128, 4, nt, nch)
            S.op(ACT, lambda e: e.activation(out=tB[ts][:, 0:nt], in_=banks[bk2][:, 0:nt], func=AF.Silu),
                 reads=[("bank", bk2)], writes=[("tB", ts)])
            S.op(POOL, lambda e: e.tensor_tensor(out=tA[ts][:, 0:nt].rearrange("p (c l) -> p c l", c=nch),
                                                 in0=tE[0][:, 0:nt].rearrange("p (c l) -> p c l", c=nch),
                                                 in1=invPL[0][:, 0:nch].unsqueeze(2).to_broadcast([128, nch, L]), op=ALU.mult),
                 reads=[("tE", 0), ("invPL", 0)], writes=[("tA", ts)])
            S.op(POOL, lambda e: e.tensor_tensor(out=QT[:, hh, 0:nt], in0=tB[ts][:, 0:nt], in1=tA[ts][:, 0:nt], op=ALU.mult),
                 reads=[("tA", ts), ("tB", ts)], writes=[("QT", bp, hh)])

        def stage_C(j, L, vs):
            bp = state["bp"]
            xT = xTs[bp]
            for seg, kind, off in ((2, "v", 0), (6, "v", 512), (3, "g", 0), (7, "g", 512)):
                bk = next_pbank()

                def mm(e, seg=seg, bk=bk):
                    ins = None
                    for kc in range(8):
                        ins = e.matmul(banks[bk][0:L, :], lhsT=xT[:, kc, j * L:(j + 1) * L], rhs=Wi[:, kc, seg * 512:(seg + 1) * 512],
                                       start=(kc == 0), stop=(kc == 7))
                    return ins
                S.op(PE, mm, reads=[("Wi", seg, q) for q in range(4)] + [("xT", bp, j)], writes=[("bank", bk)])
                if kind == "v":
                    S.op(ACT, lambda e, bk=bk, off=off: e.activation(out=V[vs][0:L, off:off + 512], in_=banks[bk][0:L, :], func=AF.Copy),
                         reads=[("bank", bk)], writes=[("V", vs, off)])
                else:
                    S.op(ACT, lambda e, bk=bk, off=off: e.activation(out=gate[vs][0:L, off:off + 512], in_=banks[bk][0:L, :],
                                                                     func=AF.Silu),
                         reads=[("bank", bk)], writes=[("gate", vs, off)])

        def stage_D(j, L, vs, has_state, sbufs=None):
            cs = slice(j * L, (j + 1) * L)
            Sst, Sb, kS0, kS1, kSb = sbufs if sbufs is not None else (Sst_p, Sb_p, "S0", "S1", "Sb")
            kS = (kS0, kS1)
            bp = state["bp"]
            QT, KT, decS = QTs[bp], KTs[bp], decSs[bp]
            if has_state:
                S.op(POOL, lambda e: e.tensor_tensor(out=Sst[:].rearrange("p (h e) -> p h e", h=8),
                                                     in0=Sst[:].rearrange("p (h e) -> p h e", h=8),
                                                     in1=decS[:, j, :].unsqueeze(2).to_broadcast([128, 8, 128]), op=ALU.mult),
                     reads=[kS0, kS1] + [("decS", bp, j, h) for h in range(8)], writes=[kS0, kS1])
                S.op(ACT, lambda e: e.activation(out=Sb[:], in_=Sst[:], func=AF.Copy), reads=[kS0, kS1], writes=[kSb])
            for g in range(2):
                hs = range(g * 4, g * 4 + 4)

                def sc(e, hs=hs):
                    ins = None
                    for hq_, h in enumerate(hs):
                        ins = e.matmul(banks[BSC][0:L, hq_ * L:(hq_ + 1) * L], lhsT=KT[:, h, cs], rhs=QT[:, h, cs], start=True, stop=True)
                    return ins
                scid = S.op(PE, sc, reads=[("KT", bp, h) for h in hs] + [("QT", bp, h) for h in hs], writes=[("bank", BSC)])
                if g == 0:
                    state["sc0"] = scid
                    if late_dma:
                        for did in late_dma:
                            S.ops[did]["deps"].add(scid)
                        del late_dma[:]
                S.op(DVE, lambda e, g=g: e.tensor_tensor(out=AT[0][0:L, g * 4:(g + 1) * 4, 0:L],
                                                         in0=banks[BSC][0:L, 0:4 * L].rearrange("p (h t) -> p h t", h=4),
                                                         in1=maskT[0:L, 0:L].unsqueeze(1).to_broadcast([L, 4, L]), op=ALU.mult),
                     reads=[("bank", BSC), "maskT"], writes=[("AT", 0, g)])

                def ktr(e, hs=hs):
                    ins = None
                    for h in hs:
                        ins = e.transpose(out=Tbf[0:L, h * 128:(h + 1) * 128], in_=KT[:, h, cs], identity=ident[:])
                    return ins
                S.op(PE, ktr, reads=[("KT", bp, h) for h in hs] + ["ident"], writes=["bT"])
                S.op(ACT, lambda e, g=g: e.activation(out=Ktm[0][0:L, g * 512:(g + 1) * 512], in_=Tbf[0:L, g * 512:(g + 1) * 512], func=AF.Copy),
                     reads=["bT"], writes=[("Ktm", 0, g)])

                def omm(e, g=g, hs=hs):
                    ins = None
                    for hq_, h in enumerate(hs):
                        o_ap = banks[BOs[g]][0:L, hq_ * 128:(hq_ + 1) * 128]
                        ins = e.matmul(o_ap, lhsT=AT[0][0:L, h, 0:L], rhs=V[vs][0:L, h * 128:(h + 1) * 128], start=True, stop=not has_state)
                        if has_state:
                            ins = e.matmul(o_ap, lhsT=QT[:, h, cs], rhs=Sb[:, h * 128:(h + 1) * 128], start=False, stop=True)
                    return ins
                S.op(PE, omm, reads=[("AT", 0, g), ("V", vs, g * 512), kSb] + [("QT", bp, h) for h in hs], writes=[("bank", BOs[g])])

                def dsm(e, hs=hs):
                    ins = None
                    for hq_, h in enumerate(hs):
                        ins = e.matmul(banks[BDS][:, hq_ * 128:(hq_ + 1) * 128], lhsT=Ktm[0][0:L, h * 128:(h + 1) * 128],
                                       rhs=V[vs][0:L, h * 128:(h + 1) * 128], start=True, stop=True)
                    return ins
                S.op(PE, dsm, reads=[("Ktm", 0, g), ("V", vs, g * 512)], writes=[("bank", BDS)])
                gsl = slice(g * 512, (g + 1) * 512)
                if has_state:
                    S.op(DVE, lambda e, gsl=gsl: e.tensor_tensor(out=Sst[:, gsl], in0=banks[BDS][:, :], in1=Sst[:, gsl], op=ALU.add),
                         reads=[("bank", BDS), kS[g], kSb], writes=[kS[g]])
                else:
                    S.op(DVE, lambda e, gsl=gsl: e.tensor_copy(out=Sst[:, gsl], in_=banks[BDS][:, :]),
                         reads=[("bank", BDS)], writes=[kS[g]])
                for hq_, h in enumerate(hs):
                    S.op(ACT, lambda e, hq_=hq_, h=h, g=g: e.activation(out=junk[0:L, hq_ * 128:(hq_ + 1) * 128], in_=banks[BOs[g]][0:L, hq_ * 128:(hq_ + 1) * 128],
                                                                   func=AF.Square, accum_out=ssqs[vs][0:L, h:h + 1]),
                         reads=[("bank", BOs[g])], writes=[("ssq", vs, h), ("junk", hq_)])
                S.op(POOL, lambda e, g=g: e.tensor_scalar(out=msqs[vs][0:L, g * 4:(g + 1) * 4], in0=ssqs[vs][0:L, g * 4:(g + 1) * 4],
                                                          scalar1=1.0 / 128, scalar2=EPS, op0=ALU.mult, op1=ALU.add),
                     reads=[("ssq", vs, h) for h in hs], writes=[("msq", vs, g)])
                S.op(POOL, lambda e, g=g: e.tensor_tensor(out=rsqs[vs][0:L, g * 4:(g + 1) * 4], in0=msqs[vs][0:L, g * 4:(g + 1) * 4],
                                                          in1=mhalf[0:L, 0:4], op=ALU.pow),
                     reads=[("msq", vs, g), "mhalf"], writes=[("rsq", vs, g)])
                for hq_, h in enumerate(hs):
                    S.op(DVE, lambda e, hq_=hq_, h=h, g=g: e.scalar_tensor_tensor(out=ogs[vs][0:L, h * 128:(h + 1) * 128],
                                                                             in0=banks[BOs[g]][0:L, hq_ * 128:(hq_ + 1) * 128],
                                                                             scalar=rsqs[vs][0:L, h:h + 1],
                                                                             in1=gate[vs][0:L, h * 128:(h + 1) * 128],
                                                                             op0=ALU.mult, op1=ALU.mult),
                         reads=[("bank", BOs[g]), ("rsq", vs, g), ("gate", vs, g * 512)], writes=[("og", vs, h)])

        def stage_E(xrows, yrows, L, slot, tag):
            vs = slot
            og = ogs[vs]
            S.op(SP, lambda e: e.dma_start(out=xr[slot][0:L, :], in_=xrows), writes=[("xr", slot)], dma_key=("xr", slot),
                 deps=[state["sc0"]])

            def tr(e):
                ins = None
                for kc in range(8):
                    ins = e.transpose(out=Tbf[:, kc * L:(kc + 1) * L], in_=og[0:L, kc * 128:(kc + 1) * 128], identity=ident[0:L, 0:L])
                return ins
            S.op(PE, tr, reads=[("og", vs, h) for h in range(8)] + ["ident"], writes=["bT"])
            S.op(ACT, lambda e: e.activation(out=ogT[:, :, 0:L], in_=Tbf[:, 0:8 * L].rearrange("p (k l) -> p k l", k=8), func=AF.Copy),
                 reads=["bT"], writes=["ogT"])
            ybk = (BY0, BR)
            for n, bk in ((0, ybk[0]), (1, ybk[1])):
                def mm(e, n=n, bk=bk):
                    ins = None
                    for kc in range(8):
                        ins = e.matmul(banks[bk][0:L, :], lhsT=ogT[:, kc, 0:L], rhs=Wo[:, kc, n * 512:(n + 1) * 512],
                                       start=(kc == 0), stop=(kc == 7))
                    return ins
                S.op(PE, mm, reads=["ogT", ("Wo", n)], writes=[("bank", bk)])
                S.op(ACT, lambda e, n=n, bk=bk: e.activation(out=og[0:L, n * 512:(n + 1) * 512], in_=banks[bk][0:L, :], func=AF.Square,
                                                             accum_out=ss2[0:L, n:n + 1]),
                     reads=[("bank", bk)], writes=[("ss2", n)] + [("og", vs, n * 4 + r) for r in range(4)])
                S.op(DVE, lambda e, n=n, bk=bk: e.tensor_tensor(out=banks[bk][0:L, :], in0=banks[bk][0:L, :],
                                                                in1=gpost[0:L, n * 512:(n + 1) * 512], op=ALU.mult),
                     reads=[("bank", bk), "gpost", ("ss2", n)], writes=[("bank", bk)])
            S.op(POOL, lambda e: e.tensor_tensor(out=ms2[0:L, :], in0=ss2[0:L, 0:1], in1=ss2[0:L, 1:2], op=ALU.add),
                 reads=[("ss2", 0), ("ss2", 1)], writes=["ms2"])
            S.op(POOL, lambda e: e.tensor_scalar(out=ms2[0:L, :], in0=ms2[0:L, :], scalar1=1.0 / D_MODEL, scalar2=EPS,
                                                op0=ALU.mult, op1=ALU.add), reads=["ms2"], writes=["ms2"])
            S.op(POOL, lambda e: e.tensor_tensor(out=rs2[0:L, :], in0=ms2[0:L, :], in1=mhalf[0:L, 0:1], op=ALU.pow),
                 reads=["ms2", "mhalf"], writes=["rs2"])
            for n, bk in ((0, ybk[0]), (1, ybk[1])):
                S.op(DVE, lambda e, n=n, bk=bk: e.scalar_tensor_tensor(out=xr[slot][0:L, n * 512:(n + 1) * 512], in0=banks[bk][0:L, :],
                                                                       scalar=rs2[0:L, 0:1], in1=xr[slot][0:L, n * 512:(n + 1) * 512],
                                                                       op0=ALU.mult, op1=ALU.add),
                     reads=[("bank", bk), "rs2", ("xr", slot)], writes=[("xr", slot)])
            sid = S.op(SP, lambda e: e.dma_start(out=yrows, in_=xr[slot][0:L, :]), reads=[("xr", slot)], dma_key=("st", slot))
            state["stores"].append(sid)

        def store_state(dst_r, dst_h, tag, Sbuf=None, keys=("S0", "S1")):
            Sbuf = Sst_p if Sbuf is None else Sbuf
            for dst, lo, nm in ((dst_r, 0, "r"), (dst_h, 4, "h")):
                sid = S.op(SP, lambda e, dst=dst, lo=lo: e.dma_start(out=dst.rearrange("h d e -> d h e"),
                                                                     in_=Sbuf[:, lo * 128:(lo + 4) * 128].rearrange("p (h e) -> p h e", h=4)),
                           reads=list(keys), dma_key=("sst", tag, nm))
                state["stores"].append(sid)

        mark("setup_done")
        BL = [4, 4, 4, 4]
        assert sum(BL) == SEQ // 128 and max(BL) <= 4
        NBP = len(BL)
        NB = NBP + 1
        TS = [sum(BL[:i]) for i in range(NBP)]
        ORDER = list(range(NB))
        PAR = {blk_: pos % 2 for pos, blk_ in enumerate(ORDER)}

        def blk(b):
            if b < NBP:
                t0 = TS[b] * 128
                return dict(L=128, nt=BL[b] * 128, nch=BL[b], xin=lambda j: xp[t0 + j * 128:t0 + (j + 1) * 128, :],
                            yout=lambda j: yp[t0 + j * 128:t0 + (j + 1) * 128, :])
            return dict(L=16, nt=32, nch=2, xin=lambda j: xsm[j * 16:(j + 1) * 16, :], yout=lambda j: ys[j * 16:(j + 1) * 16, :])

        def A_pieces(b):
            d = blk(b)
            out = []
            for j in range(d["nch"]):
                def f(j=j, d=d):
                    state["bp"] = PAR[b]
                    if b == 0 and j >= 2:
                        stage_A(d["xin"](j), d["L"], j % 2, j, xbuf=xr[j % 2], xkey=("xr", j % 2))
                    else:
                        stage_A(d["xin"](j), d["L"], j % 2, j)
                out.append(f)
            return out

        decstate = {}

        def B_pieces(b):
            d = blk(b)
            out = []

            def pre():
                state["bp"] = PAR[b]
                if b < NBP:
                    if decstate.get(PAR[b]) != 128:
                        set_ret_dec(0, 4, PAR[b])
                        decstate[PAR[b]] = 128
                    S.op(SP, lambda e: e.dma_start(out=rotb[0][:, :, 0:d["nt"]], in_=rot_p[:, :, TS[b] * 128:TS[b] * 128 + d["nt"]]),
                         writes=[("rot", 0)], dma_key=("rot", 0))
                else:
                    set_ret_dec(1, 2, PAR[b])
                    decstate[PAR[b]] = 16
                    S.op(POOL, lambda e: e.memset(tF[0][:, 0:32], 1.0), writes=[("tF", 0)])
                    S.op(SP, lambda e: e.dma_start(out=rotb[0][:, :, 0:32], in_=rot_s), writes=[("rot", 0)], dma_key=("rot", 0))
            out.append(pre)
            for h in range(4):
                for which in (0, 1):
                    def f(h=h, which=which):
                        state["bp"] = PAR[b]
                        stage_B_ret(h, which, d["nt"], d["L"], d["nch"], 0)
                    out.append(f)
            for h in range(4):
                def f(h=h):
                    state["bp"] = PAR[b]
                    stage_B_hg(h, d["nt"], d["L"], d["nch"])
                out.append(f)
            return out

        def tile_parts(b, j):
            d = blk(b)
            vs = j % 2

            def fC():
                state["bp"] = PAR[b]
                stage_C(j, d["L"], vs)

            def fD():
                state["bp"] = PAR[b]
                if b == NBP:
                    sbufs = (xs[j], xbf[j], ("xs", j), ("xs", j), ("xbf", j))
                    for src, lo, nm in ((sret, 0, "r"), (shg, 4, "h")):
                        S.op(SP, lambda e, src=src, lo=lo, j=j: e.dma_start(out=xs[j][:, lo * 128:(lo + 4) * 128].rearrange("p (h e) -> p h e", h=4),
                                                                             in_=src[j].rearrange("h d e -> d h e")),
                             writes=[("xs", j)], dma_key=("sld", nm, j))
                    stage_D(j, d["L"], vs, has_state=True, sbufs=sbufs)
                    store_state(nrs[j], nhs[j], ("s", j), Sbuf=xs[j], keys=(("xs", j),))
                else:
                    stage_D(j, d["L"], vs, has_state=not (b == 0 and j == 0))
                    if b == NBP - 1 and j == BL[-1] - 1:
                        store_state(nrp, nhp, "p")

            def fE():
                state["bp"] = PAR[b]
                stage_E(d["xin"](j), d["yout"](j), d["L"], vs, (b, j))
            return fC, fD, fE

        PAT = "C.D.E."
        nlast = BL[-1]
        SI0 = max(0, nlast - 2)
        SI1 = nlast - 1
        ptiles = [(b, j) for b in range(NBP) for j in range(BL[b])]
        parts = {t: tile_parts(*t) for t in ptiles + [(NBP, 0), (NBP, 1)]}
        for f in A_pieces(0):
            f()
        for f in B_pieces(0):
            f()

        def emit_tile(t):
            fC, fD, fE = parts[t]
            for ch in PAT:
                if ch == "C":
                    fC()
                elif ch == "D":
                    fD()
                elif ch == "E":
                    fE()
                elif ch == "." and t[0] < NBP:
                    k = state["slot_i"]
                    if k < len(state["pending"]):
                        for f in state["pending"][k]:
                            f()
                    state["slot_i"] = k + 1

        state["pending"], state["slot_i"] = [], 0
        for (b, j) in ptiles:
            if j == 0:
                nxt = A_pieces(b + 1) + B_pieces(b + 1)
                ntl = BL[b] if b < NBP - 1 else max(1, SI0)
                nslots = ntl * PAT.count(".")
                pend = [[] for _ in range(nslots)]
                for k, f in enumerate(nxt):
                    pend[min(nslots - 1, k * nslots // len(nxt))].append(f)
                state["pending"], state["slot_i"] = pend, 0
                if b == 0:
                    nfront = 0
                    flat = [f for grp in pend for f in grp]
                    for f in flat[:nfront]:
                        f()
                    rest = flat[nfront:]
                    pend = [[] for _ in range(nslots)]
                    for k, f in enumerate(rest):
                        pend[min(nslots - 1, k * nslots // max(1, len(rest)))].append(f)
                    state["pending"] = pend
            emit_tile((b, j))
            if b == NBP - 1 and j == SI0:
                emit_tile((NBP, 0))
            if b == NBP - 1 and j == SI1:
                emit_tile((NBP, 1))

        if limit is not None:
            S.ops = S.ops[:limit]
            state["stores"] = [i for i in state["stores"] if i < limit]
        S.op(SP, lambda e: e.nop(), deps=state["stores"])

        def filler_fn(e):
            return e.matmul(banks[BR][:, 0:128], lhsT=ident[:], rhs=ident[:], start=True, stop=True)
        S.emit(nc, st, filler=(filler_fn, ident_ready, ("bank", BR), 700.0, 400.0, 60.0, 80000.0))
    return nc


def _host_tables():
    d = 128
    inv = (1.0 / (np.float32(10000.0) ** (np.arange(0, d, 2, dtype=np.float32) / np.float32(d)))).astype(np.float32)

    def rot(pos):
        ang = (pos.astype(np.float32)[:, None] * inv[None, :]).astype(np.float32)
        c = np.cos(ang).astype(np.float32).T
        s = np.sin(ang).astype(np.float32).T
        C = np.concatenate([c, c], axis=0)
        Sg = np.concatenate([-s, s], axis=0)
        return np.ascontiguousarray(np.stack([C, Sg], axis=1)).astype(np.float32)

    rot_p = rot(np.arange(SEQ, dtype=np.float32))
    ps = PAST_LEN + np.arange(DEC_SEQ, dtype=np.float32)
    rot_s = rot(np.concatenate([ps, ps]))
    lg = np.log((1.0 - 2.0 ** (-5.0 - np.arange(4, dtype=np.float32))).astype(np.float32)).astype(np.float64)

    def dec(L):
        t = np.arange(L, dtype=np.float64)
        gq = np.exp(lg[:, None] * (t[None, :] + 1.0 - L))
        gk = np.exp(lg[:, None] * (L - 1.0 - t[None, :])) * (128.0 ** -0.5)
        tab = np.stack([gq, gk], axis=0).astype(np.float32)
        return np.ascontiguousarray(np.broadcast_to(tab[None], (128, 2, 4, L))).astype(np.float32)

    glv = np.stack([np.exp(lg * 128.0), np.exp(lg * 16.0)], axis=0).astype(np.float32)
    gl = np.ascontiguousarray(np.broadcast_to(glv[None], (128, 2, 4))).astype(np.float32)
    return dict(rot_p=rot_p, rot_s=rot_s, dec128=dec(128), dec16=dec(16), gl=gl)


_NC_CACHE = {}


def kernel(x_prompt, x_sample, state_ret, state_hgrn, w_in, w_out, g_pre, g_post, lower_bounds):
    f32 = lambda a: np.ascontiguousarray(np.asarray(a, dtype=np.float32))
    x_prompt, x_sample = f32(x_prompt), f32(x_sample)
    state_ret, state_hgrn = f32(state_ret), f32(state_hgrn)
    w_in, w_out, g_pre, g_post, lower_bounds = f32(w_in), f32(w_out), f32(g_pre), f32(g_post), f32(lower_bounds)
    if "nc" not in _NC_CACHE:
        _NC_CACHE["nc"] = build_nc()
        _NC_CACHE["tabs"] = _host_tables()
    nc = _NC_CACHE["nc"]
    tabs = _NC_CACHE["tabs"]
    in_maps = []
    for c in range(N_CORES):
        m = dict(tabs)
        m["xp"] = x_prompt[c]
        m["xsm"] = np.ascontiguousarray(x_sample[2 * c:2 * c + 2].reshape(2 * DEC_SEQ, D_MODEL))
        m["sret"] = np.ascontiguousarray(state_ret[0, 2 * c:2 * c + 2])
        m["shg"] = np.ascontiguousarray(state_hgrn[0, 2 * c:2 * c + 2])
        m["w_in"] = w_in[0]
        m["w_out"] = w_out[0]
        m["g_pre"] = g_pre
        m["g_post"] = g_post
        m["lbs"] = np.ascontiguousarray(lower_bounds.reshape(2, 4, 128).transpose(2, 0, 1)).reshape(128, 8)
        in_maps.append(m)
    res = run_bass_kernel_spmd(nc, in_maps, core_ids=list(range(N_CORES)))
    r = res.results
    y_p = np.stack([r[c]["yp"] for c in range(N_CORES)], axis=0)
    y_s = np.concatenate([r[c]["ys"].reshape(2, DEC_SEQ, D_MODEL) for c in range(N_CORES)], axis=0)
    nrp = np.stack([r[c]["nrp"] for c in range(N_CORES)], axis=0)[None]
    nrs = np.concatenate([r[c]["nrs"] for c in range(N_CORES)], axis=0)[None]
    nhp = np.stack([r[c]["nhp"] for c in range(N_CORES)], axis=0)[None]
    nhs = np.concatenate([r[c]["nhs"] for c in range(N_CORES)], axis=0)[None]
    return (y_p.astype(np.float32), y_s.astype(np.float32), nrp.astype(np.float32), nrs.astype(np.float32),
            nhp.astype(np.float32), nhs.astype(np.float32))
```

```python
import sys
import numpy as np
from contextlib import ExitStack
import concourse.bass as bass
import concourse.mybir as mybir
from concourse.bass_utils import run_bass_kernel_spmd

F32 = mybir.dt.float32
BF16 = mybir.dt.bfloat16
ALU = mybir.AluOpType
AF = mybir.ActivationFunctionType

PE, ACT, DVE, POOL, SP = "pe", "act", "dve", "pool", "sp"
ENGS = (PE, ACT, DVE, POOL, SP)

D_MODEL = 1024
SEQ = 2048
DEC_SEQ = 16
PAST_LEN = 4096
EPS = 1e-6
N_CORES = 8


class Sched:
    def __init__(self):
        self.ops = []
        self.last_w = {}
        self.readers = {}

    def op(self, eng, fn, reads=(), writes=(), dma_key=None, deps=()):
        oid = len(self.ops)
        d = set(deps)
        for b in reads:
            w = self.last_w.get(b)
            if w is not None:
                d.add(w)
        for b in writes:
            w = self.last_w.get(b)
            if w is not None:
                d.add(w)
            d.update(self.readers.get(b, ()))
        d.discard(oid)
        self.ops.append(dict(eng=eng, fn=fn, deps=d, dma_key=dma_key, sig=False, sem=None, val=0, line=sys._getframe(1).f_lineno,
                             writes=tuple(writes)))
        for b in reads:
            self.readers.setdefault(b, []).append(oid)
        for b in writes:
            self.last_w[b] = oid
            self.readers[b] = []
        return oid

    def estimate_costs(self):
        class _Ins:
            def then_inc(self, *a, **k):
                return self

        class _Fake:
            def __init__(self, eng):
                self.eng = eng
                self.cost = 0.0
                self.lat = 0.0
                self.nbytes = 0

            def __getattr__(self, name):
                def f(*args, **kw):
                    out = kw.get("out", args[0] if args else None)
                    n = 1
                    try:
                        n = int(out.free_size())
                    except Exception:
                        pass
                    e = self.eng
                    if name == "dma_start":
                        try:
                            nbytes = int(out.partition_size()) * n * 4
                        except Exception:
                            nbytes = 4096
                        self.cost += 1200.0 if e == POOL else 120.0
                        self.lat += 2000.0
                        self.nbytes += nbytes
                    elif e == PE:
                        self.cost += 240.0 if n >= 512 else 20.0 + 0.37 * n
                    elif e == ACT:
                        self.cost += 190.0 + 0.833 * n + (91.0 if kw.get("accum_out") is not None else 0.0)
                    elif e == DVE:
                        if name == "tensor_tensor_scan":
                            self.cost += 250.0 + 2.08 * n
                        elif name == "reciprocal":
                            self.cost += 150.0 + 8.3 * n
                        elif name in ("tensor_scalar", "tensor_copy"):
                            self.cost += max(250.0, 170.0 + 0.7 * n)
                        else:
                            self.cost += max(250.0, 130.0 + 1.0 * n)
                    elif e == POOL:
                        if kw.get("op") == ALU.pow:
                            self.cost += 740.0
                        elif name in ("memset", "affine_select"):
                            self.cost += 200.0 + 0.5 * n
                        else:
                            self.cost += 150.0 + 1.9 * n
                    else:
                        self.cost += 50.0
                    return _Ins()
                return f

        for o in self.ops:
            fk = _Fake(o["eng"])
            o["fn"](fk)
            o["cost"] = max(fk.cost, 20.0)
            o["lat"] = fk.lat
            o["nbytes"] = fk.nbytes

    def list_schedule(self):
        PRIO_MODE = 0
        ops = self.ops
        n = len(ops)
        self.estimate_costs()
        succ = [[] for _ in range(n)]
        for i, o in enumerate(ops):
            for d in o["deps"]:
                succ[d].append(i)
        prio = [0.0] * n
        for i in range(n - 1, -1, -1):
            o = ops[i]
            best = 0.0
            for sx in succ[i]:
                if prio[sx] > best:
                    best = prio[sx]
            if PRIO_MODE == 1:
                prio[i] = best + (o["cost"] if o["eng"] == PE else 0.25 * (o["cost"] + o["lat"]))
            elif PRIO_MODE == 2:
                prio[i] = -float(i)
            else:
                prio[i] = best + o["cost"] + o["lat"]
        for i, o in enumerate(ops):
            prio[i] += o.get("boost", 0.0)
        XLAT = 250.0
        WIN = 300.0
        ndeps = [len(o["deps"]) for o in ops]
        ready_t = [0.0] * n
        avail = {e: [] for e in ENGS}
        for i, o in enumerate(ops):
            if ndeps[i] == 0:
                avail[o["eng"]].append(i)
        free_t = {e: 0.0 for e in ENGS}
        dma_free = [0.0, 0.0]
        order = {e: [] for e in ENGS}
        done = 0
        while done < n:
            best = None
            for e in ENGS:
                if not avail[e]:
                    continue
                cand = None
                for i in avail[e]:
                    st_ = max(free_t[e], ready_t[i])
                    key = (st_, -prio[i], i)
                    if cand is None or key < cand[0]:
                        cand = (key, i, st_)
                win = cand[2] + WIN
                for i in avail[e]:
                    st_ = max(free_t[e], ready_t[i])
                    if st_ <= win and prio[i] > prio[cand[1]]:
                        cand = ((st_, -prio[i], i), i, st_)
                if best is None or cand[2] < best[2] or (cand[2] == best[2] and prio[cand[1]] > prio[best[1]]):
                    best = (e, cand[1], cand[2])
            e, i, st_ = best
            o = ops[i]
            avail[e].remove(i)
            fin = st_ + o["cost"]
            o["t_idle"] = st_ - free_t[e]
            o["t_start"], o["t_fin"] = st_, fin
            free_t[e] = fin
            order[e].append(i)
            done += 1
            dlat = o["lat"]
            if o["dma_key"] is not None:
                ch = 0 if e == POOL else 1
                xfer_start = max(fin, dma_free[ch])
                dma_free[ch] = xfer_start + o["nbytes"] / 260.0
                dlat = (dma_free[ch] - fin) + o["lat"]
            for sx in succ[i]:
                lat = dlat + (XLAT if ops[sx]["eng"] != e or o["dma_key"] is not None else 60.0)
                if fin + lat > ready_t[sx]:
                    ready_t[sx] = fin + lat
                ndeps[sx] -= 1
                if ndeps[sx] == 0:
                    avail[ops[sx]["eng"]].append(sx)
        self.est_makespan = max(free_t.values())
        return order

    def add_fillers(self, order, filler):
        fn, dep_op, bank_key, gap_min, margin, fdur, t_max = filler
        ops = self.ops
        pe = order[PE]
        stop_at = None
        for k, i in enumerate(pe):
            if bank_key in ops[i]["writes"]:
                stop_at = k
                break
        if stop_at is None:
            stop_at = len(pe)
        new = []
        prev_fin = ops[dep_op]["t_fin"] + 400.0
        nfill = 0
        for k, i in enumerate(pe):
            if k <= stop_at and ops[i]["t_start"] < t_max:
                gap = ops[i]["t_start"] - prev_fin
                if gap > gap_min:
                    cnt = int((gap - margin) / fdur)
                    for _ in range(max(0, cnt)):
                        fid = len(ops)
                        ops.append(dict(eng=PE, fn=fn, deps={dep_op}, dma_key=None, sig=False, sem=None, val=0, line=0,
                                        writes=(bank_key,), cost=fdur, lat=0.0, nbytes=0))
                        new.append(fid)
                        nfill += 1
                prev_fin = max(prev_fin, ops[i]["t_fin"])
            new.append(i)
        order[PE] = new
        self.n_fillers = nfill
        return order

    def emit(self, nc, stack, reorder=True, filler=None):
        ops = self.ops
        if reorder:
            order = self.list_schedule()
            if filler is not None:
                order = self.add_fillers(order, filler)
        else:
            order = {e: [i for i, o in enumerate(ops) if o["eng"] == e] for e in ENGS}
        pos = {}
        for e in ENGS:
            for k, i in enumerate(order[e]):
                pos[i] = k
        need = [set() for _ in ops]
        for i, o in enumerate(ops):
            for dpi in o["deps"]:
                dp = ops[dpi]
                if dp["dma_key"] is None and dp["eng"] == o["eng"] and o["dma_key"] is None:
                    if o["eng"] == PE:
                        continue
                need[i].add(dpi)
                dp["sig"] = True
        esem = {e: stack.enter_context(nc.semaphore("sem_" + e)) for e in (PE, ACT, DVE, POOL)}
        dsem, dcount = {}, {}
        ecount = {e: 0 for e in ENGS}
        for i, o in enumerate(ops):
            if o["dma_key"] is not None:
                k = o["dma_key"]
                if k not in dsem:
                    dsem[k] = stack.enter_context(nc.semaphore("dsem%d" % len(dsem)))
                    dcount[k] = 0
                dcount[k] += 16
                o["sem"], o["val"], o["sig"] = dsem[k], dcount[k], True
        for e in ENGS:
            for i in order[e]:
                o = ops[i]
                if o["dma_key"] is None and o["sig"]:
                    ecount[e] += 1
                    o["sem"], o["val"] = esem[e], ecount[e]
        streams = order

        def run(h, e):
            waited = {}
            for i in streams[e]:
                o = ops[i]
                req = {}
                for dpi in need[i]:
                    dp = ops[dpi]
                    key = id(dp["sem"])
                    if key not in req or req[key][1] < dp["val"]:
                        req[key] = (dp["sem"], dp["val"])
                for key, (sem, val) in req.items():
                    if waited.get(key, 0) >= val:
                        continue
                    h.wait_ge(sem, val)
                    waited[key] = val
                ins = o["fn"](h)
                if o["sig"]:
                    ins.then_inc(o["sem"], 16 if o["dma_key"] is not None else 1)

        block = stack.enter_context(nc.Block())

        @block.tensor
        def _(e):
            run(e, PE)

        @block.scalar
        def _(e):
            run(e, ACT)

        @block.vector
        def _(e):
            run(e, DVE)

        @block.gpsimd
        def _(e):
            run(e, POOL)

        @block.sync
        def _(e):
            run(e, SP)


def rsl(start, n):
    stop = start - n
    return slice(start, stop if stop >= 0 else None, -1)


def build_nc(limit=None, marks=None):
    nc = bass.Bass("TRN2", target_bir_lowering=False)

    def din(name, shape):
        return nc.dram_tensor(name, list(shape), F32, kind="ExternalInput").ap()

    def dout(name, shape):
        return nc.dram_tensor(name, list(shape), F32, kind="ExternalOutput").ap()

    xp = din("xp", [SEQ, D_MODEL])
    xsm = din("xsm", [2 * DEC_SEQ, D_MODEL])
    sret = din("sret", [2, 4, 128, 128])
    shg = din("shg", [2, 4, 128, 128])
    w_in = din("w_in", [1024, 4096])
    w_out = din("w_out", [1024, 1024])
    g_pre = din("g_pre", [1, 1024])
    g_post = din("g_post", [1, 1024])
    lbs = din("lbs", [128, 8])
    rot_p = din("rot_p", [128, 2, SEQ])
    rot_s = din("rot_s", [128, 2, 2 * DEC_SEQ])
    dec128 = din("dec128", [128, 2, 4, 128])
    dec16 = din("dec16", [128, 2, 4, 16])
    gl = din("gl", [128, 2, 4])
    yp = dout("yp", [SEQ, D_MODEL])
    ys = dout("ys", [2 * DEC_SEQ, D_MODEL])
    nrp = dout("nrp", [4, 128, 128])
    nhp = dout("nhp", [4, 128, 128])
    nrs = dout("nrs", [2, 4, 128, 128])
    nhs = dout("nhs", [2, 4, 128, 128])

    with ExitStack() as st:
        def sb(name, shape, dt):
            return st.enter_context(nc.sbuf_tensor(name, list(shape), dt))

        Wi = sb("Wi", [128, 8, 4096], BF16)
        Wo = sb("Wo", [128, 8, 1024], BF16)
        gpre = sb("gpre", [128, 1024], F32)
        gpost = sb("gpost", [128, 1024], F32)
        rotb = [sb("rotb%d" % i, [128, 2, 512], F32) for i in range(1)]
        decT = {128: sb("decT128", [128, 2, 4, 128], F32), 16: sb("decT16", [128, 2, 4, 16], F32)}
        glT = sb("glT", [128, 2, 4], F32)
        lbt = sb("lbt", [128, 2, 4], F32)
        lbd = sb("lbd", [128, 4], F32)
        lbc = sb("lbc", [128, 3, 4], F32)
        ident = sb("ident", [128, 128], BF16)
        perm = sb("perm", [128, 128], BF16)
        maskT = sb("maskT", [128, 128], F32)
        zer = sb("zer", [128, 128], BF16)
        mhalf = sb("mhalf", [128, 8], F32)
        xs = [sb("xs%d" % i, [128, 1024], F32) for i in range(2)]
        xr = [sb("xr%d" % i, [128, 1024], F32) for i in range(2)]
        xbf = [sb("xbf%d" % i, [128, 1024], BF16) for i in range(2)]
        junk = sb("junk", [128, 512], BF16)
        permb = junk[:, 0:128]
        ssx = sb("ssx", [128, 2], F32)
        msx = sb("msx", [128, 2], F32)
        rsx = sb("rsx", [128, 2], F32)
        xTs = [sb("xT%d" % i, [128, 8, 512], BF16) for i in range(2)]
        NTS = 2
        tA = [sb("tA%d" % i, [128, 512], F32) for i in range(NTS)]
        tB = [sb("tB%d" % i, [128, 512], F32) for i in range(NTS)]
        tC = [sb("tC%d" % i, [128, 512], F32) for i in range(1)]
        tE = [sb("tE%d" % i, [128, 512], F32) for i in range(1)]
        tF = [sb("tF%d" % i, [128, 512], F32) for i in range(1)]
        qb = [sb("qb%d" % i, [128, 512], BF16) for i in range(NTS)]
        invPL = [sb("invPL%d" % i, [128, 4], F32) for i in range(1)]
        QTs = [sb("QT%d" % i, [128, 8, 512], BF16) for i in range(2)]
        KTs = [sb("KT%d" % i, [128, 8, 512], BF16) for i in range(2)]
        decSs = [sb("decS%d" % i, [128, 4, 8], F32) for i in range(2)]
        V = [sb("V%d" % i, [128, 1024], BF16) for i in range(2)]
        gate = [sb("gate%d" % i, [128, 1024], BF16) for i in range(2)]
        Ktm = [sb("Ktm%d" % i, [128, 1024], BF16) for i in range(1)]
        AT = [sb("AT%d" % i, [128, 8, 128], BF16) for i in range(1)]
        Sst_p = sb("Sst", [128, 1024], F32)
        Sb_p = sb("Sb", [128, 1024], BF16)
        ssqs = [sb("ssq%d" % i, [128, 8], F32) for i in range(2)]
        msqs = [sb("msq%d" % i, [128, 8], F32) for i in range(2)]
        rsqs = [sb("rsq%d" % i, [128, 8], F32) for i in range(2)]
        ogs = [sb("og%d" % i, [128, 1024], BF16) for i in range(2)]
        ogT = sb("ogT", [128, 8, 128], BF16)
        ss2 = sb("ss2", [128, 2], F32)
        ms2 = sb("ms2", [128, 1], F32)
        rs2 = sb("rs2", [128, 1], F32)
        banks = [st.enter_context(nc.psum_tensor("bank%d" % i, [128, 512], F32)) for i in range(8)]
        BT, BP0, BP1, BR, BSC, BO0, BO1, BY0 = range(8)
        BDS = BSC
        BOs = (BO0, BO1)
        BY1 = BR
        Tbf = banks[BT][:].bitcast(BF16)

        S = Sched()
        state = dict(pbank=0, tset=0, stores=[], bp=0)

        PPOOL = (BP0, BP1)

        def next_pbank():
            state["pbank"] = (state["pbank"] + 1) % len(PPOOL)
            return PPOOL[state["pbank"]]

        def next_tset():
            state["tset"] = (state["tset"] + 1) % NTS
            return state["tset"]

        S.op(POOL, lambda e: e.memset(ident[:], 1.0), writes=["ident"])
        ident_ready = S.op(POOL, lambda e: e.affine_select(out=ident[:], in_=ident[:], pattern=[[-1, 128]], compare_op=ALU.is_equal,
                                                           fill=0.0, base=0, channel_multiplier=1), reads=["ident"], writes=["ident"])
        S.op(POOL, lambda e: e.memset(perm[:], 1.0), writes=["perm"])
        S.op(POOL, lambda e: e.affine_select(out=perm[:], in_=perm[:], pattern=[[-1, 128]], compare_op=ALU.is_equal,
                                             fill=0.0, base=-64, channel_multiplier=1), reads=["perm"], writes=["perm"])
        S.op(POOL, lambda e: e.memset(permb, 1.0), writes=["permb"] + [("junk", r) for r in range(4)])
        S.op(POOL, lambda e: e.affine_select(out=permb, in_=permb, pattern=[[-1, 128]], compare_op=ALU.is_equal,
                                             fill=0.0, base=64, channel_multiplier=1), reads=["permb"], writes=["permb"] + [("junk", r) for r in range(4)])
        S.op(POOL, lambda e: e.tensor_tensor(out=perm[:], in0=perm[:], in1=permb, op=ALU.add),
             reads=["perm", "permb"] + [("junk", r) for r in range(4)], writes=["perm"])
        S.op(POOL, lambda e: e.memset(maskT[:], 1.0), writes=["maskT"])
        S.op(POOL, lambda e: e.affine_select(out=maskT[:], in_=maskT[:], pattern=[[1, 128]], compare_op=ALU.is_ge,
                                             fill=0.0, base=0, channel_multiplier=-1), reads=["maskT"], writes=["maskT"])
        S.op(POOL, lambda e: e.memset(zer[:], 0.0), writes=["zer"])
        S.op(POOL, lambda e: e.memset(mhalf[:], -0.5), writes=["mhalf"])
        for i in range(1):
            S.op(POOL, lambda e, i=i: e.memset(tF[i][:], 1.0), writes=[("tF", i)])
        w_in_v = w_in.rearrange("(kc p) e -> p kc e", p=128)
        late_w = []
        all_w = []
        WIN_W = 6
        for k, seg in enumerate((0, 1, 5, 4, 2, 3, 6, 7)):
            pieces = ((0, 256), (256, 512)) if k < 2 else ((0, 512),)
            for (c0, c1) in pieces:
                wid = S.op(POOL, lambda e, seg=seg, c0=c0, c1=c1: e.dma_start(out=Wi[:, :, seg * 512 + c0:seg * 512 + c1],
                                                                               in_=w_in_v[:, :, seg * 512 + c0:seg * 512 + c1]),
                           writes=[("Wi", seg, q) for q in range(c0 // 128, c1 // 128)], dma_key=("Wi", seg, c0),
                           deps=all_w[-WIN_W:-WIN_W + 1] if len(all_w) >= WIN_W else [])
                all_w.append(wid)
                if k >= 2:
                    late_w.append(wid)
        w_out_v = w_out.rearrange("(kc p) e -> p kc e", p=128)
        for n in range(2):
            wid = S.op(POOL, lambda e, n=n: e.dma_start(out=Wo[:, :, n * 512:(n + 1) * 512], in_=w_out_v[:, :, n * 512:(n + 1) * 512]),
                       writes=[("Wo", n)], dma_key=("Wo", n), deps=all_w[-WIN_W:-WIN_W + 1])
            all_w.append(wid)
            late_w.append(wid)
        gid = S.op(SP, lambda e: e.dma_start(out=gpre[:], in_=g_pre.partition_broadcast(128)), writes=["gpre"], dma_key="gpre")
        S.ops[gid]["boost"] = 4.5e6
        late_dma = []
        late_dma.append(S.op(SP, lambda e: e.dma_start(out=gpost[:], in_=g_post.partition_broadcast(128)), writes=["gpost"], dma_key="gpost"))
        S.op(SP, lambda e: e.dma_start(out=decT[128][:], in_=dec128), writes=[("dec", 128)], dma_key="dec128")
        late_dma.append(S.op(SP, lambda e: e.dma_start(out=decT[16][:], in_=dec16), writes=[("dec", 16)], dma_key="dec16"))
        S.op(SP, lambda e: e.dma_start(out=glT[:], in_=gl), writes=["glT"], dma_key="glT")

        S.op(SP, lambda e: e.dma_start(out=lbt[:], in_=lbs.rearrange("p (r h) -> p r h", r=2)), writes=["lbt"], dma_key="lbt")
        S.op(DVE, lambda e: e.tensor_tensor(out=lbd[:], in0=lbt[:, 0, :], in1=lbt[:, 1, :], op=ALU.subtract),
             reads=["lbt"], writes=["lbd"])
        S.op(ACT, lambda e: e.activation(out=lbd[:], in_=lbd[:], func=AF.Tanh, scale=0.5), reads=["lbd"], writes=["lbd"])
        S.op(DVE, lambda e: e.tensor_scalar(out=lbc[:, 0, :], in0=lbd[:], scalar1=0.25, scalar2=0.75, op0=ALU.mult, op1=ALU.add),
             reads=["lbd"], writes=["lbc0"])
        S.op(DVE, lambda e: e.tensor_scalar(out=lbc[:, 1, :], in0=lbd[:], scalar1=-0.25, scalar2=0.25, op0=ALU.mult, op1=ALU.add),
             reads=["lbd"], writes=["lbc1"])
        S.op(DVE, lambda e: e.tensor_scalar(out=lbc[:, 2, :], in0=lbd[:], scalar1=0.25, scalar2=-0.25, op0=ALU.mult, op1=ALU.add),
             reads=["lbd"], writes=["lbc2"])
        LBC = ["lbc0", "lbc1", "lbc2"]

        def mark(name):
            if marks is not None:
                marks[name] = len(S.ops)

        def set_ret_dec(li, nch, bp):
            decS = decSs[bp]
            for c in range(nch):
                S.op(DVE, lambda e, c=c: e.tensor_copy(out=decS[:, c, 0:4], in_=glT[:, li, :]),
                     reads=["glT"], writes=[("decS", bp, c, h) for h in range(4)])

        def stage_A(xrows, L, slot, j, xbuf=None, xkey=None):
            bp = state["bp"]
            xT = xTs[bp]
            if xbuf is None:
                xbuf, xkey = xs[slot], ("xs", slot)
            xid = S.op(SP, lambda e: e.dma_start(out=xbuf[0:L, :], in_=xrows), writes=[xkey], dma_key=xkey)
            state["nx"] = state.get("nx", 0) + 1
            if state["nx"] <= 4:
                S.ops[xid]["boost"] = 5e6 - 2e5 * (state["nx"] - 1) if state["nx"] == 1 else 4.4e6 - 1e5 * state["nx"]
            S.op(ACT, lambda e: e.activation(out=xbf[slot][0:L, :], in_=xbuf[0:L, :], func=AF.Square,
                                             accum_out=ssx[0:L, slot:slot + 1]),
                 reads=[xkey], writes=[("ssx", slot), ("xbf", slot)])
            S.op(POOL, lambda e: e.tensor_scalar(out=msx[0:L, slot:slot + 1], in0=ssx[0:L, slot:slot + 1],
                                                scalar1=1.0 / D_MODEL, scalar2=EPS, op0=ALU.mult, op1=ALU.add),
                 reads=[("ssx", slot)], writes=[("msx", slot)])
            pid = S.op(POOL, lambda e: e.tensor_tensor(out=rsx[0:L, slot:slot + 1], in0=msx[0:L, slot:slot + 1],
                                                       in1=mhalf[0:L, 0:1], op=ALU.pow),
                       reads=[("msx", slot), "mhalf"], writes=[("rsx", slot)])
            state["npow"] = state.get("npow", 0) + 1
            if state["npow"] <= 2:
                for k, wid in enumerate(late_w):
                    if (state["npow"] == 1 and k < 4) or (state["npow"] == 2 and k >= 4):
                        S.ops[wid]["deps"].add(pid)
            S.op(DVE, lambda e: e.scalar_tensor_tensor(out=xbf[slot][0:L, :], in0=xbuf[0:L, :], scalar=rsx[0:L, slot:slot + 1],
                                                       in1=gpre[0:L, :], op0=ALU.mult, op1=ALU.mult),
                 reads=[xkey, ("rsx", slot), "gpre"], writes=[("xbf", slot)])

            def tr(e):
                ins = None
                for kc in range(8):
                    ins = e.transpose(out=Tbf[:, kc * L:(kc + 1) * L], in_=xbf[slot][0:L, kc * 128:(kc + 1) * 128],
                                      identity=ident[0:L, 0:L])
                return ins
            S.op(PE, tr, reads=[("xbf", slot), "ident"], writes=["bT"])
            S.op(ACT, lambda e: e.activation(out=xT[:, :, j * L:(j + 1) * L],
                                             in_=Tbf[:, 0:8 * L].rearrange("p (k l) -> p k l", k=8), func=AF.Copy),
                 reads=["bT"], writes=[("xT", bp, j)])

        def proj_fm(col0, seg, nt, nch):
            bk = next_pbank()
            bp = state["bp"]
            xT = xTs[bp]

            def mm(e):
                ins = None
                for kc in range(8):
                    ins = e.matmul(banks[bk][:, 0:nt], lhsT=Wi[:, kc, col0:col0 + 128], rhs=xT[:, kc, 0:nt],
                                   start=(kc == 0), stop=(kc == 7))
                return ins
            S.op(PE, mm, reads=[("Wi", seg, (col0 % 512) // 128)] + [("xT", bp, j) for j in range(nch)], writes=[("bank", bk)])
            return bk

        def stage_B_ret(h, which, nt, L, nch, rslot):
            seg = which
            bk = proj_fm(seg * 512 + h * 128, seg, nt, nch)
            ts = next_tset()
            bp = state["bp"]
            dst = QTs[bp] if which == 0 else KTs[bp]
            dkey = ("QT", bp, h) if which == 0 else ("KT", bp, h)
            br = bk
            gtab = decT[L][:, which, h, 0:L].unsqueeze(1).to_broadcast([128, nch, L])
            S.op(DVE, lambda e: e.tensor_tensor(out=qb[ts][:, 0:nt].rearrange("p (c l) -> p c l", c=nch),
                                                in0=banks[bk][:, 0:nt].rearrange("p (c l) -> p c l", c=nch),
                                                in1=gtab, op=ALU.mult),
                 reads=[("bank", bk), ("dec", L)], writes=[("qb", ts)])
            S.op(PE, lambda e: e.matmul(banks[br][:, 0:nt], lhsT=perm[:], rhs=qb[ts][:, 0:nt], start=True, stop=True),
                 reads=[("qb", ts), "perm"], writes=[("bank", br)])
            S.op(POOL, lambda e: e.tensor_tensor(out=tA[ts][:, 0:nt], in0=qb[ts][:, 0:nt], in1=rotb[rslot][:, 0, 0:nt], op=ALU.mult),
                 reads=[("qb", ts), ("rot", rslot)], writes=[("tA", ts)])
            S.op(DVE, lambda e: e.tensor_tensor(out=tB[ts][:, 0:nt], in0=banks[br][:, 0:nt], in1=rotb[rslot][:, 1, 0:nt], op=ALU.mult),
                 reads=[("bank", br), ("rot", rslot)], writes=[("tB", ts)])
            S.op(POOL, lambda e: e.tensor_tensor(out=dst[:, h, 0:nt], in0=tA[ts][:, 0:nt], in1=tB[ts][:, 0:nt], op=ALU.add),
                 reads=[("tA", ts), ("tB", ts)], writes=[dkey])

        def stage_B_hg(h, nt, L, nch):
            hh = 4 + h
            ts = next_tset()
            bp = state["bp"]
            QT, KT, decS = QTs[bp], KTs[bp], decSs[bp]
            bk = proj_fm(5 * 512 + h * 128, 5, nt, nch)
            S.op(ACT, lambda e: e.activation(out=tC[0][:, 0:nt], in_=banks[bk][:, 0:nt], func=AF.Tanh, scale=0.5),
                 reads=[("bank", bk)], writes=[("tC", 0)])
            S.op(DVE, lambda e: e.tensor_scalar(out=tC[0][:, 0:nt], in0=tC[0][:, 0:nt], scalar1=lbc[:, 1, h:h + 1],
                                                scalar2=lbc[:, 0, h:h + 1], op0=ALU.mult, op1=ALU.add),
                 reads=[("tC", 0)] + LBC, writes=[("tC", 0)])
            for c in range(nch):
                S.op(DVE, lambda e, c=c: e.tensor_tensor_scan(out=tE[0][:, c * L:(c + 1) * L], data0=tC[0][:, c * L:(c + 1) * L],
                                                              data1=zer[:, 0:L], initial=1.0, op0=ALU.mult, op1=ALU.add),
                     reads=[("tC", 0), "zer"], writes=[("tE", 0)])
            for c in range(nch):
                S.op(DVE, lambda e, c=c: e.tensor_tensor_scan(out=tF[0][:, rsl(c * L + L - 2, L - 1)],
                                                              data0=tC[0][:, rsl(c * L + L - 1, L - 1)],
                                                              data1=zer[:, 0:L - 1], initial=1.0, op0=ALU.mult, op1=ALU.add),
                     reads=[("tC", 0), "zer"], writes=[("tF", 0)])
            S.op(DVE, lambda e: e.tensor_scalar(out=tC[0][:, 0:nt], in0=tC[0][:, 0:nt], scalar1=-1.0,
                                                scalar2=1.0, op0=ALU.mult, op1=ALU.add),
                 reads=[("tC", 0)] + LBC, writes=[("tC", 0)])
            S.op(POOL, lambda e: e.tensor_tensor(out=KT[:, hh, 0:nt], in0=tC[0][:, 0:nt], in1=tF[0][:, 0:nt], op=ALU.mult),
                 reads=[("tC", 0), ("tF", 0)], writes=[("KT", bp, hh)])
            S.op(DVE, lambda e: e.tensor_copy(out=decS[:, 0:nch, hh],
                                              in_=tE[0][:, 0:nt].rearrange("p (c l) -> p c l", c=nch)[:, :, L - 1]),
                 reads=[("tE", 0)], writes=[("decS", bp, c, hh) for c in range(nch)])
            S.op(DVE, lambda e: e.reciprocal(out=invPL[0][:, 0:nch], in_=decS[:, 0:nch, hh]),
                 reads=[("decS", bp, c, hh) for c in range(nch)], writes=[("invPL", 0)])
            bk2 = proj_fm(4 * 512 + h * 128, 4, nt, nch)
            S.op(ACT, lambda e: e.activation(out=tB[ts][:, 0:nt], in_=banks[bk2][:, 0:nt], func=AF.Silu),
                 reads=[("bank", bk2)], writes=[("tB", ts)])
            S.op(POOL, lambda e: e.tensor_tensor(out=tA[ts][:, 0:nt].rearrange("p (c l) -> p c l", c=nch),
                                                 in0=tE[0][:, 0:nt].rearrange("p (c l) -> p c l", c=nch),
                                                 in1=invPL[0][:, 0:nch].unsqueeze(2).to_broadcast([128, nch, L]), op=ALU.mult),
                 reads=[("tE", 0), ("invPL", 0)], writes=[("tA", ts)])
            S.op(POOL, lambda e: e.tensor_tensor(out=QT[:, hh, 0:nt], in0=tB[ts][:, 0:nt], in1=tA[ts][:, 0:nt], op=ALU.mult),
                 reads=[("tA", ts), ("tB", ts)], writes=[("QT", bp, hh)])

        def stage_C(j, L, vs):
            bp = state["bp"]
            xT = xTs[bp]
            for seg, kind, off in ((2, "v", 0), (6, "v", 512), (3, "g", 0), (7, "g", 512)):
                bk = next_pbank()

                def mm(e, seg=seg, bk=bk):
                    ins = None
                    for kc in range(8):
                        ins = e.matmul(banks[bk][0:L, :], lhsT=xT[:, kc, j * L:(j + 1) * L], rhs=Wi[:, kc, seg * 512:(seg + 1) * 512],
                                       start=(kc == 0), stop=(kc == 7))
                    return ins
                S.op(PE, mm, reads=[("Wi", seg, q) for q in range(4)] + [("xT", bp, j)], writes=[("bank", bk)])
                if kind == "v":
                    S.op(ACT, lambda e, bk=bk, off=off: e.activation(out=V[vs][0:L, off:off + 512], in_=banks[bk][0:L, :], func=AF.Copy),
                         reads=[("bank", bk)], writes=[("V", vs, off)])
                else:
                    S.op(ACT, lambda e, bk=bk, off=off: e.activation(out=gate[vs][0:L, off:off + 512], in_=banks[bk][0:L, :],
                                                                     func=AF.Silu),
                         reads=[("bank", bk)], writes=[("gate", vs, off)])

        def stage_D(j, L, vs, has_state, sbufs=None):
            cs = slice(j * L, (j + 1) * L)
            Sst, Sb, kS0, kS1, kSb = sbufs if sbufs is not None else (Sst_p, Sb_p, "S0", "S1", "Sb")
            kS = (kS0, kS1)
            bp = state["bp"]
            QT, KT, decS = QTs[bp], KTs[bp], decSs[bp]
            if has_state:
                S.op(POOL, lambda e: e.tensor_tensor(out=Sst[:].rearrange("p (h e) -> p h e", h=8),
                                                     in0=Sst[:].rearrange("p (h e) -> p h e", h=8),
                                                     in1=decS[:, j, :].unsqueeze(2).to_broadcast([128, 8, 128]), op=ALU.mult),
                     reads=[kS0, kS1] + [("decS", bp, j, h) for h in range(8)], writes=[kS0, kS1])
                S.op(ACT, lambda e: e.activation(out=Sb[:], in_=Sst[:], func=AF.Copy), reads=[kS0, kS1], writes=[kSb])
            for g in range(2):
                hs = range(g * 4, g * 4 + 4)

                def sc(e, hs=hs):
                    ins = None
                    for hq_, h in enumerate(hs):
                        ins = e.matmul(banks[BSC][0:L, hq_ * L:(hq_ + 1) * L], lhsT=KT[:, h, cs], rhs=QT[:, h, cs], start=True, stop=True)
                    return ins
                scid = S.op(PE, sc, reads=[("KT", bp, h) for h in hs] + [("QT", bp, h) for h in hs], writes=[("bank", BSC)])
                if g == 0:
                    state["sc0"] = scid
                    if late_dma:
                        for did in late_dma:
                            S.ops[did]["deps"].add(scid)
                        del late_dma[:]
                S.op(DVE, lambda e, g=g: e.tensor_tensor(out=AT[0][0:L, g * 4:(g + 1) * 4, 0:L],
                                                         in0=banks[BSC][0:L, 0:4 * L].rearrange("p (h t) -> p h t", h=4),
                                                         in1=maskT[0:L, 0:L].unsqueeze(1).to_broadcast([L, 4, L]), op=ALU.mult),
                     reads=[("bank", BSC), "maskT"], writes=[("AT", 0, g)])

                def ktr(e, hs=hs):
                    ins = None
                    for h in hs:
                        ins = e.transpose(out=Tbf[0:L, h * 128:(h + 1) * 128], in_=KT[:, h, cs], identity=ident[:])
                    return ins
                S.op(PE, ktr, reads=[("KT", bp, h) for h in hs] + ["ident"], writes=["bT"])
                S.op(ACT, lambda e, g=g: e.activation(out=Ktm[0][0:L, g * 512:(g + 1) * 512], in_=Tbf[0:L, g * 512:(g + 1) * 512], func=AF.Copy),
                     reads=["bT"], writes=[("Ktm", 0, g)])

                def omm(e, g=g, hs=hs):
                    ins = None
                    for hq_, h in enumerate(hs):
                        o_ap = banks[BOs[g]][0:L, hq_ * 128:(hq_ + 1) * 128]
                        ins = e.matmul(o_ap, lhsT=AT[0][0:L, h, 0:L], rhs=V[vs][0:L, h * 128:(h + 1) * 128], start=True, stop=not has_state)
                        if has_state:
                            ins = e.matmul(o_ap, lhsT=QT[:, h, cs], rhs=Sb[:, h * 128:(h + 1) * 128], start=False, stop=True)
                    return ins
                S.op(PE, omm, reads=[("AT", 0, g), ("V", vs, g * 512), kSb] + [("QT", bp, h) for h in hs], writes=[("bank", BOs[g])])

                def dsm(e, hs=hs):
                    ins = None
                    for hq_, h in enumerate(hs):
                        ins = e.matmul(banks[BDS][:, hq_ * 128:(hq_ + 1) * 128], lhsT=Ktm[0][0:L, h * 128:(h + 1) * 128],
                                       rhs=V[vs][0:L, h * 128:(h + 1) * 128], start=True, stop=True)
                    return ins
                S.op(PE, dsm, reads=[("Ktm", 0, g), ("V", vs, g * 512)], writes=[("bank", BDS)])
                gsl = slice(g * 512, (g + 1) * 512)
                if has_state:
                    S.op(DVE, lambda e, gsl=gsl: e.tensor_tensor(out=Sst[:, gsl], in0=banks[BDS][:, :], in1=Sst[:, gsl], op=ALU.add),
                         reads=[("bank", BDS), kS[g], kSb], writes=[kS[g]])
                else:
                    S.op(DVE, lambda e, gsl=gsl: e.tensor_copy(out=Sst[:, gsl], in_=banks[BDS][:, :]),
                         reads=[("bank", BDS)], writes=[kS[g]])
                for hq_, h in enumerate(hs):
                    S.op(ACT, lambda e, hq_=hq_, h=h, g=g: e.activation(out=junk[0:L, hq_ * 128:(hq_ + 1) * 128], in_=banks[BOs[g]][0:L, hq_ * 128:(hq_ + 1) * 128],
                                                                   func=AF.Square, accum_out=ssqs[vs][0:L, h:h + 1]),
                         reads=[("bank", BOs[g])], writes=[("ssq", vs, h), ("junk", hq_)])
                S.op(POOL, lambda e, g=g: e.tensor_scalar(out=msqs[vs][0:L, g * 4:(g + 1) * 4], in0=ssqs[vs][0:L, g * 4:(g + 1) * 4],
                                                          scalar1=1.0 / 128, scalar2=EPS, op0=ALU.mult, op1=ALU.add),
                     reads=[("ssq", vs, h) for h in hs], writes=[("msq", vs, g)])
                S.op(POOL, lambda e, g=g: e.tensor_tensor(out=rsqs[vs][0:L, g * 4:(g + 1) * 4], in0=msqs[vs][0:L, g * 4:(g + 1) * 4],
                                                          in1=mhalf[0:L, 0:4], op=ALU.pow),
                     reads=[("msq", vs, g), "mhalf"], writes=[("rsq", vs, g)])
                for hq_, h in enumerate(hs):
                    S.op(DVE, lambda e, hq_=hq_, h=h, g=g: e.scalar_tensor_tensor(out=ogs[vs][0:L, h * 128:(h + 1) * 128],
                                                                             in0=banks[BOs[g]][0:L, hq_ * 128:(hq_ + 1) * 128],
                                                                             scalar=rsqs[vs][0:L, h:h + 1],
                                                                             in1=gate[vs][0:L, h * 128:(h + 1) * 128],
                                                                             op0=ALU.mult, op1=ALU.mult),
                         reads=[("bank", BOs[g]), ("rsq", vs, g), ("gate", vs, g * 512)], writes=[("og", vs, h)])

        def stage_E(xrows, yrows, L, slot, tag):
            vs = slot
            og = ogs[vs]
            S.op(SP, lambda e: e.dma_start(out=xr[slot][0:L, :], in_=xrows), writes=[("xr", slot)], dma_key=("xr", slot),
                 deps=[state["sc0"]])

            def tr(e):
                ins = None
                for kc in range(8):
                    ins = e.transpose(out=Tbf[:, kc * L:(kc + 1) * L], in_=og[0:L, kc * 128:(kc + 1) * 128], identity=ident[0:L, 0:L])
                return ins
            S.op(PE, tr, reads=[("og", vs, h) for h in range(8)] + ["ident"], writes=["bT"])
            S.op(ACT, lambda e: e.activation(out=ogT[:, :, 0:L], in_=Tbf[:, 0:8 * L].rearrange("p (k l) -> p k l", k=8), func=AF.Copy),
                 reads=["bT"], writes=["ogT"])
            ybk = (BY0, BR)
            for n, bk in ((0, ybk[0]), (1, ybk[1])):
                def mm(e, n=n, bk=bk):
                    ins = None
                    for kc in range(8):
                        ins = e.matmul(banks[bk][0:L, :], lhsT=ogT[:, kc, 0:L], rhs=Wo[:, kc, n * 512:(n + 1) * 512],
                                       start=(kc == 0), stop=(kc == 7))
                    return ins
                S.op(PE, mm, reads=["ogT", ("Wo", n)], writes=[("bank", bk)])
                S.op(ACT, lambda e, n=n, bk=bk: e.activation(out=og[0:L, n * 512:(n + 1) * 512], in_=banks[bk][0:L, :], func=AF.Square,
                                                             accum_out=ss2[0:L, n:n + 1]),
                     reads=[("bank", bk)], writes=[("ss2", n)] + [("og", vs, n * 4 + r) for r in range(4)])
                S.op(DVE, lambda e, n=n, bk=bk: e.tensor_tensor(out=banks[bk][0:L, :], in0=banks[bk][0:L, :],
                                                                in1=gpost[0:L, n * 512:(n + 1) * 512], op=ALU.mult),
                     reads=[("bank", bk), "gpost", ("ss2", n)], writes=[("bank", bk)])
            S.op(POOL, lambda e: e.tensor_tensor(out=ms2[0:L, :], in0=ss2[0:L, 0:1], in1=ss2[0:L, 1:2], op=ALU.add),
                 reads=[("ss2", 0), ("ss2", 1)], writes=["ms2"])
            S.op(POOL, lambda e: e.tensor_scalar(out=ms2[0:L, :], in0=ms2[0:L, :], scalar1=1.0 / D_MODEL, scalar2=EPS,
                                                op0=ALU.mult, op1=ALU.add), reads=["ms2"], writes=["ms2"])
            S.op(POOL, lambda e: e.tensor_tensor(out=rs2[0:L, :], in0=ms2[0:L, :], in1=mhalf[0:L, 0:1], op=ALU.pow),
                 reads=["ms2", "mhalf"], writes=["rs2"])
            for n, bk in ((0, ybk[0]), (1, ybk[1])):
                S.op(DVE, lambda e, n=n, bk=bk: e.scalar_tensor_tensor(out=xr[slot][0:L, n * 512:(n + 1) * 512], in0=banks[bk][0:L, :],
                                                                       scalar=rs2[0:L, 0:1], in1=xr[slot][0:L, n * 512:(n + 1) * 512],
                                                                       op0=ALU.mult, op1=ALU.add),
                     reads=[("bank", bk), "rs2", ("xr", slot)], writes=[("xr", slot)])
            sid = S.op(SP, lambda e: e.dma_start(out=yrows, in_=xr[slot][0:L, :]), reads=[("xr", slot)], dma_key=("st", slot))
            state["stores"].append(sid)

        def store_state(dst_r, dst_h, tag, Sbuf=None, keys=("S0", "S1")):
            Sbuf = Sst_p if Sbuf is None else Sbuf
            for dst, lo, nm in ((dst_r, 0, "r"), (dst_h, 4, "h")):
                sid = S.op(SP, lambda e, dst=dst, lo=lo: e.dma_start(out=dst.rearrange("h d e -> d h e"),
                                                                     in_=Sbuf[:, lo * 128:(lo + 4) * 128].rearrange("p (h e) -> p h e", h=4)),
                           reads=list(keys), dma_key=("sst", tag, nm))
                state["stores"].append(sid)

        mark("setup_done")
        BL = [4, 4, 4, 4]
        assert sum(BL) == SEQ // 128 and max(BL) <= 4
        NBP = len(BL)
        NB = NBP + 1
        TS = [sum(BL[:i]) for i in range(NBP)]
        ORDER = list(range(NB))
        PAR = {blk_: pos % 2 for pos, blk_ in enumerate(ORDER)}

        def blk(b):
            if b < NBP:
                t0 = TS[b] * 128
                return dict(L=128, nt=BL[b] * 128, nch=BL[b], xin=lambda j: xp[t0 + j * 128:t0 + (j + 1) * 128, :],
                            yout=lambda j: yp[t0 + j * 128:t0 + (j + 1) * 128, :])
            return dict(L=16, nt=32, nch=2, xin=lambda j: xsm[j * 16:(j + 1) * 16, :], yout=lambda j: ys[j * 16:(j + 1) * 16, :])

        def A_pieces(b):
            d = blk(b)
            out = []
            for j in range(d["nch"]):
                def f(j=j, d=d):
                    state["bp"] = PAR[b]
                    if b == 0 and j >= 2:
                        stage_A(d["xin"](j), d["L"], j % 2, j, xbuf=xr[j % 2], xkey=("xr", j % 2))
                    else:
                        stage_A(d["xin"](j), d["L"], j % 2, j)
                out.append(f)
            return out

        decstate = {}

        def B_pieces(b):
            d = blk(b)
            out = []

            def pre():
                state["bp"] = PAR[b]
                if b < NBP:
                    if decstate.get(PAR[b]) != 128:
                        set_ret_dec(0, 4, PAR[b])
                        decstate[PAR[b]] = 128
                    S.op(SP, lambda e: e.dma_start(out=rotb[0][:, :, 0:d["nt"]], in_=rot_p[:, :, TS[b] * 128:TS[b] * 128 + d["nt"]]),
                         writes=[("rot", 0)], dma_key=("rot", 0))
                else:
                    set_ret_dec(1, 2, PAR[b])
                    decstate[PAR[b]] = 16
                    S.op(POOL, lambda e: e.memset(tF[0][:, 0:32], 1.0), writes=[("tF", 0)])
                    S.op(SP, lambda e: e.dma_start(out=rotb[0][:, :, 0:32], in_=rot_s), writes=[("rot", 0)], dma_key=("rot", 0))
            out.append(pre)
            for h in range(4):
                for which in (0, 1):
                    def f(h=h, which=which):
                        state["bp"] = PAR[b]
                        stage_B_ret(h, which, d["nt"], d["L"], d["nch"], 0)
                    out.append(f)
            for h in range(4):
                def f(h=h):
                    state["bp"] = PAR[b]
                    stage_B_hg(h, d["nt"], d["L"], d["nch"])
                out.append(f)
            return out

        def tile_parts(b, j):
            d = blk(b)
            vs = j % 2

            def fC():
                state["bp"] = PAR[b]
                stage_C(j, d["L"], vs)

            def fD():
                state["bp"] = PAR[b]
                if b == NBP:
                    sbufs = (xs[j], xbf[j], ("xs", j), ("xs", j), ("xbf", j))
                    for src, lo, nm in ((sret, 0, "r"), (shg, 4, "h")):
                        S.op(SP, lambda e, src=src, lo=lo, j=j: e.dma_start(out=xs[j][:, lo * 128:(lo + 4) * 128].rearrange("p (h e) -> p h e", h=4),
                                                                             in_=src[j].rearrange("h d e -> d h e")),
                             writes=[("xs", j)], dma_key=("sld", nm, j))
                    stage_D(j, d["L"], vs, has_state=True, sbufs=sbufs)
                    store_state(nrs[j], nhs[j], ("s", j), Sbuf=xs[j], keys=(("xs", j),))
                else:
                    stage_D(j, d["L"], vs, has_state=not (b == 0 and j == 0))
                    if b == NBP - 1 and j == BL[-1] - 1:
                        store_state(nrp, nhp, "p")

            def fE():
                state["bp"] = PAR[b]
                stage_E(d["xin"](j), d["yout"](j), d["L"], vs, (b, j))
            return fC, fD, fE

        PAT = "C.D.E."
        nlast = BL[-1]
        SI0 = max(0, nlast - 2)
        SI1 = nlast - 1
        ptiles = [(b, j) for b in range(NBP) for j in range(BL[b])]
        parts = {t: tile_parts(*t) for t in ptiles + [(NBP, 0), (NBP, 1)]}
        for f in A_pieces(0):
            f()
        for f in B_pieces(0):
            f()

        def emit_tile(t):
            fC, fD, fE = parts[t]
            for ch in PAT:
                if ch == "C":
                    fC()
                elif ch == "D":
                    fD()
                elif ch == "E":
                    fE()
                elif ch == "." and t[0] < NBP:
                    k = state["slot_i"]
                    if k < len(state["pending"]):
                        for f in state["pending"][k]:
                            f()
                    state["slot_i"] = k + 1

        state["pending"], state["slot_i"] = [], 0
        for (b, j) in ptiles:
            if j == 0:
                nxt = A_pieces(b + 1) + B_pieces(b + 1)
                ntl = BL[b] if b < NBP - 1 else max(1, SI0)
                nslots = ntl * PAT.count(".")
                pend = [[] for _ in range(nslots)]
                for k, f in enumerate(nxt):
                    pend[min(nslots - 1, k * nslots // len(nxt))].append(f)
                state["pending"], state["slot_i"] = pend, 0
                if b == 0:
                    nfront = 0
                    flat = [f for grp in pend for f in grp]
                    for f in flat[:nfront]:
                        f()
                    rest = flat[nfront:]
                    pend = [[] for _ in range(nslots)]
                    for k, f in enumerate(rest):
                        pend[min(nslots - 1, k * nslots // max(1, len(rest)))].append(f)
                    state["pending"] = pend
            emit_tile((b, j))
            if b == NBP - 1 and j == SI0:
                emit_tile((NBP, 0))
            if b == NBP - 1 and j == SI1:
                emit_tile((NBP, 1))

        if limit is not None:
            S.ops = S.ops[:limit]
            state["stores"] = [i for i in state["stores"] if i < limit]
        S.op(SP, lambda e: e.nop(), deps=state["stores"])

        def filler_fn(e):
            return e.matmul(banks[BR][:, 0:128], lhsT=ident[:], rhs=ident[:], start=True, stop=True)
        S.emit(nc, st, filler=(filler_fn, ident_ready, ("bank", BR), 700.0, 400.0, 60.0, 30000.0))
    return nc


def _host_tables():
    d = 128
    inv = (1.0 / (np.float32(10000.0) ** (np.arange(0, d, 2, dtype=np.float32) / np.float32(d)))).astype(np.float32)

    def rot(pos):
        ang = (pos.astype(np.float32)[:, None] * inv[None, :]).astype(np.float32)
        c = np.cos(ang).astype(np.float32).T
        s = np.sin(ang).astype(np.float32).T
        C = np.concatenate([c, c], axis=0)
        Sg = np.concatenate([-s, s], axis=0)
        return np.ascontiguousarray(np.stack([C, Sg], axis=1)).astype(np.float32)

    rot_p = rot(np.arange(SEQ, dtype=np.float32))
    ps = PAST_LEN + np.arange(DEC_SEQ, dtype=np.float32)
    rot_s = rot(np.concatenate([ps, ps]))
    lg = np.log((1.0 - 2.0 ** (-5.0 - np.arange(4, dtype=np.float32))).astype(np.float32)).astype(np.float64)

    def dec(L):
        t = np.arange(L, dtype=np.float64)
        gq = np.exp(lg[:, None] * (t[None, :] + 1.0 - L))
        gk = np.exp(lg[:, None] * (L - 1.0 - t[None, :])) * (128.0 ** -0.5)
        tab = np.stack([gq, gk], axis=0).astype(np.float32)
        return np.ascontiguousarray(np.broadcast_to(tab[None], (128, 2, 4, L))).astype(np.float32)

    glv = np.stack([np.exp(lg * 128.0), np.exp(lg * 16.0)], axis=0).astype(np.float32)
    gl = np.ascontiguousarray(np.broadcast_to(glv[None], (128, 2, 4))).astype(np.float32)
    return dict(rot_p=rot_p, rot_s=rot_s, dec128=dec(128), dec16=dec(16), gl=gl)


_NC_CACHE = {}


def kernel(x_prompt, x_sample, state_ret, state_hgrn, w_in, w_out, g_pre, g_post, lower_bounds):
    f32 = lambda a: np.ascontiguousarray(np.asarray(a, dtype=np.float32))
    x_prompt, x_sample = f32(x_prompt), f32(x_sample)
    state_ret, state_hgrn = f32(state_ret), f32(state_hgrn)
    w_in, w_out, g_pre, g_post, lower_bounds = f32(w_in), f32(w_out), f32(g_pre), f32(g_post), f32(lower_bounds)
    if "nc" not in _NC_CACHE:
        _NC_CACHE["nc"] = build_nc()
        _NC_CACHE["tabs"] = _host_tables()
    nc = _NC_CACHE["nc"]
    tabs = _NC_CACHE["tabs"]
    in_maps = []
    for c in range(N_CORES):
        m = dict(tabs)
        m["xp"] = x_prompt[c]
        m["xsm"] = np.ascontiguousarray(x_sample[2 * c:2 * c + 2].reshape(2 * DEC_SEQ, D_MODEL))
        m["sret"] = np.ascontiguousarray(state_ret[0, 2 * c:2 * c + 2])
        m["shg"] = np.ascontiguousarray(state_hgrn[0, 2 * c:2 * c + 2])
        m["w_in"] = w_in[0]
        m["w_out"] = w_out[0]
        m["g_pre"] = g_pre
        m["g_post"] = g_post
        m["lbs"] = np.ascontiguousarray(lower_bounds.reshape(2, 4, 128).transpose(2, 0, 1)).reshape(128, 8)
        in_maps.append(m)
    res = run_bass_kernel_spmd(nc, in_maps, core_ids=list(range(N_CORES)))
    r = res.results
    y_p = np.stack([r[c]["yp"] for c in range(N_CORES)], axis=0)
    y_s = np.concatenate([r[c]["ys"].reshape(2, DEC_SEQ, D_MODEL) for c in range(N_CORES)], axis=0)
    nrp = np.stack([r[c]["nrp"] for c in range(N_CORES)], axis=0)[None]
    nrs = np.concatenate([r[c]["nrs"] for c in range(N_CORES)], axis=0)[None]
    nhp = np.stack([r[c]["nhp"] for c in range(N_CORES)], axis=0)[None]
    nhs = np.concatenate([r[c]["nhs"] for c in range(N_CORES)], axis=0)[None]
    return (y_p.astype(np.float32), y_s.astype(np.float32), nrp.astype(np.float32), nrs.astype(np.float32),
            nhp.astype(np.float32), nhs.astype(np.float32))
```
